# Optimizing a Trainium2 kernel written in Bass

```python
import math
import jax
import jax.numpy as jnp
from jax import lax
import numpy as np

D_MODEL = 1024
BATCH = 8
SEQ = 4096
DEPTH = 1

A_HEADS = 16
A_HEAD_DIM = 64
A_WIDTH = A_HEADS * A_HEAD_DIM
DILATED_PATTERNS = ((128, 1), (512, 4), (2048, 16))
N_BUCKETS = 32
MAX_EXACT = N_BUCKETS // 2
MAX_DISTANCE = 2048
M_HEADS = 4
M_WIDTH = 2 * D_MODEL
M_HEAD_DIM = M_WIDTH // M_HEADS
CONV_K = 4
QKV_BLOCK = 4
CHUNK = 64
EPS = 1e-6
N_IN = 4 * A_WIDTH + 3 * M_WIDTH + 2 * D_MODEL
SPLIT_POINTS = (A_WIDTH, 2 * A_WIDTH, 3 * A_WIDTH, 4 * A_WIDTH, 4 * A_WIDTH + M_WIDTH, 4 * A_WIDTH + 2 * M_WIDTH, 4 * A_WIDTH + 3 * M_WIDTH)

kernel_name = 'hybrid_dilated_attn_mlstm_gated'


def rmsnorm(x, g):
    xf = x.astype(jnp.float32)
    y = xf * lax.rsqrt(jnp.mean(xf * xf, axis=-1, keepdims=True) + EPS)
    return (y * g.astype(jnp.float32)).astype(x.dtype)


def t5_bucket(dist):
    large = MAX_EXACT + (jnp.log(jnp.maximum(dist, MAX_EXACT).astype(jnp.float32) / MAX_EXACT)
                         / math.log(MAX_DISTANCE / MAX_EXACT) * (N_BUCKETS - MAX_EXACT)).astype(jnp.int32)
    return jnp.where(dist < MAX_EXACT, dist, jnp.minimum(large, N_BUCKETS - 1))


def dilated_band_attention(q, k, v, rel_bias, window, dilation):
    B, S, H, dh = q.shape
    band = window // dilation
    span = band * dilation
    s_pad = -(-S // span) * span
    L = s_pad // dilation
    nb = L // band

    def to_blocks(t):
        t = jnp.pad(t, ((0, 0), (0, s_pad - S), (0, 0), (0, 0)))
        t = t.reshape(B, L, dilation, H, dh).transpose(0, 2, 3, 1, 4)
        return t.reshape(B, dilation, H, nb, band, dh)

    def with_prev(t):
        prev = jnp.pad(t, ((0, 0), (0, 0), (0, 0), (1, 0), (0, 0), (0, 0)))[:, :, :, :-1]
        return jnp.concatenate([prev, t], axis=4)

    qb = to_blocks(q)
    kc = with_prev(to_blocks(k))
    vc = with_prev(to_blocks(v))
    logits = jnp.einsum('brhnqc,brhnkc->brhnqk', qb, kc, preferred_element_type=jnp.float32) * (dh ** -0.5)
    qi = jnp.arange(band)[:, None]
    kj = jnp.arange(2 * band)[None, :]
    delta = qi + band - kj
    bias = rel_bias.astype(jnp.float32)[t5_bucket(jnp.clip(delta, 0, band) * dilation)]
    bias = jnp.transpose(bias, (2, 0, 1))[:, None]
    key_pos = jnp.arange(nb)[:, None, None] * band + kj[None] - band
    valid = (delta >= 0) & (delta <= band) & (key_pos >= 0)
    logits = jnp.where(valid, logits + bias, -jnp.inf)
    m = jnp.max(logits, axis=-1)
    p = jnp.exp(logits - m[..., None])
    s = jnp.sum(p, axis=-1)
    o = jnp.einsum('brhnqk,brhnkc->brhnqc', p, vc.astype(jnp.float32)) / s[..., None]

    def from_blocks(t):
        t = t.reshape((B, dilation, H, L) + t.shape[5:])
        t = jnp.moveaxis(t, 3, 1)
        return t.reshape((B, s_pad, H) + t.shape[4:])[:, :S]

    return from_blocks(o), from_blocks(m), from_blocks(s)


def dilated_attention(q, k, v, rel_bias):
    B, S, H, dh = q.shape
    outs = [dilated_band_attention(q, k, v, rel_bias, w, d) for (w, d) in DILATED_PATTERNS]
    o = jnp.stack([t[0] for t in outs])
    m = jnp.stack([t[1] for t in outs])
    s = jnp.stack([t[2] for t in outs])
    wgt = jnp.exp(m - jnp.max(m, axis=0, keepdims=True)) * s
    out = jnp.sum(wgt[..., None] * o, axis=0) / jnp.sum(wgt, axis=0)[..., None]
    return out.reshape(B, S, H * dh)


def causal_depthwise_conv(x, w, b):
    S = x.shape[1]
    xp = jnp.pad(x, ((0, 0), (CONV_K - 1, 0), (0, 0)))
    y = b
    for tap in range(CONV_K):
        y = y + xp[:, tap:tap + S] * w[tap]
    return y


def block_diag_proj(x, w):
    B, S, C = x.shape
    return jnp.einsum('bsgi,gio->bsgo', x.reshape(B, S, C // QKV_BLOCK, QKV_BLOCK), w).reshape(B, S, C)


def mlstm_chunkwise(q, k, v, li, lf):
    B, S, H, dh = q.shape
    nc = S // CHUNK

    def chunks(t):
        t = t.astype(jnp.float32).reshape((B, nc, CHUNK, H) + t.shape[3:])
        return jnp.moveaxis(jnp.moveaxis(t, 1, 0), 3, 2)

    qc, kc, vc = chunks(q), chunks(k) * (dh ** -0.5), chunks(v)
    lic, lfc = chunks(li), chunks(lf)
    tril = jnp.tril(jnp.ones((CHUNK, CHUNK), dtype=bool))

    def step(carry, xs):
        C, n, m = carry
        qt, kt, vt, it, ft = xs
        b = jnp.cumsum(ft, axis=-1)
        g = b[..., -1]
        D = jnp.where(tril, b[..., :, None] - b[..., None, :] + it[..., None, :], -jnp.inf)
        inter = b + m[..., None]
        m_t = jnp.maximum(inter, jnp.max(D, axis=-1))
        qk = jnp.einsum('bhtd,bhsd->bhts', qt, kt) * jnp.exp(D - m_t[..., None])
        w_inter = jnp.exp(inter - m_t)
        num = w_inter[..., None] * jnp.einsum('bhtd,bhde->bhte', qt, C) + jnp.einsum('bhts,bhse->bhte', qk, vt)
        den = w_inter * jnp.einsum('bhtd,bhd->bht', qt, n) + jnp.sum(qk, axis=-1)
        h = num / jnp.maximum(jnp.abs(den), jnp.exp(-m_t))[..., None]
        to_end = g[..., None] - b + it
        m_new = jnp.maximum(g + m, jnp.max(to_end, axis=-1))
        w_s = jnp.exp(to_end - m_new[..., None])
        decay = jnp.exp(g + m - m_new)
        C_new = decay[..., None, None] * C + jnp.einsum('bhsd,bhse->bhde', kt * w_s[..., None], vt)
        n_new = decay[..., None] * n + jnp.einsum('bhs,bhsd->bhd', w_s, kt)
        return (C_new, n_new, m_new), h

    init = (jnp.zeros((B, H, dh, dh), jnp.float32), jnp.zeros((B, H, dh), jnp.float32), jnp.zeros((B, H), jnp.float32))
    _, h = lax.scan(step, init, (qc, kc, vc, lic, lfc))
    return jnp.transpose(h, (1, 0, 3, 2, 4)).reshape(B, S, H, dh)


def mlstm_branch(x_m, z_m, o_m, conv_w, conv_b, wq, wk, wv, w_if, b_if, head_norm_g, skip):
    B, S, _ = x_m.shape
    x_c = jax.nn.silu(causal_depthwise_conv(x_m, conv_w, conv_b))
    q = block_diag_proj(x_c, wq)
    k = block_diag_proj(x_c, wk)
    v = block_diag_proj(x_m, wv)
    gate_pre = (jnp.concatenate([q, k, v], axis=-1) @ w_if + b_if).astype(jnp.float32)
    li = gate_pre[..., :M_HEADS]
    lf = jax.nn.log_sigmoid(gate_pre[..., M_HEADS:])
    heads = lambda t: t.reshape(B, S, M_HEADS, M_HEAD_DIM)
    h = mlstm_chunkwise(heads(q), heads(k), heads(v), li, lf)
    h = jax.nn.sigmoid(heads(o_m).astype(jnp.float32)) * h
    mu = jnp.mean(h, axis=-1, keepdims=True)
    var = jnp.mean(jnp.square(h - mu), axis=-1, keepdims=True)
    h = (h - mu) * lax.rsqrt(var + EPS) * head_norm_g.astype(jnp.float32).reshape(M_HEADS, M_HEAD_DIM)
    h = h.reshape(B, S, M_WIDTH) + skip * x_c
    return h * jax.nn.silu(z_m)


def setup_inputs(seed: int = 0) -> dict:
    key = jax.random.key(seed)
    ks = jax.random.split(key, 20)
    nrm = lambda k, shape, scale: jax.random.normal(k, shape, jnp.float32) * scale
    n_blocks = M_WIDTH // QKV_BLOCK
    x = nrm(ks[0], (BATCH, SEQ, D_MODEL), 1.0)
    norm_in_g = 1.0 + nrm(ks[1], (DEPTH, D_MODEL), 0.02)
    w_in = nrm(ks[2], (DEPTH, D_MODEL, N_IN), D_MODEL ** -0.5)
    gate_b = nrm(ks[3], (DEPTH, 2 * D_MODEL), 0.01)
    conv_w = nrm(ks[4], (DEPTH, CONV_K, M_WIDTH), CONV_K ** -0.5)
    conv_b = nrm(ks[5], (DEPTH, M_WIDTH), 0.01)
    wq_m = nrm(ks[6], (DEPTH, n_blocks, QKV_BLOCK, QKV_BLOCK), QKV_BLOCK ** -0.5)
    wk_m = nrm(ks[7], (DEPTH, n_blocks, QKV_BLOCK, QKV_BLOCK), QKV_BLOCK ** -0.5)
    wv_m = nrm(ks[8], (DEPTH, n_blocks, QKV_BLOCK, QKV_BLOCK), QKV_BLOCK ** -0.5)
    w_if = nrm(ks[9], (DEPTH, 3 * M_WIDTH, 2 * M_HEADS), (3 * M_WIDTH) ** -0.5)
    b_if = jnp.concatenate([nrm(ks[10], (DEPTH, M_HEADS), 0.1),
                            jnp.linspace(3.0, 6.0, M_HEADS)[None] + nrm(ks[11], (DEPTH, M_HEADS), 0.01)], axis=-1)
    head_norm_g = 1.0 + nrm(ks[12], (DEPTH, M_WIDTH), 0.02)
    skip_m = 1.0 + nrm(ks[13], (DEPTH, M_WIDTH), 0.02)
    w_pa = nrm(ks[14], (DEPTH, A_WIDTH, D_MODEL), A_WIDTH ** -0.5)
    w_pb = nrm(ks[15], (DEPTH, M_WIDTH, D_MODEL), M_WIDTH ** -0.5)
    w_out = nrm(ks[16], (DEPTH, D_MODEL, D_MODEL), D_MODEL ** -0.5)
    rel_bias = nrm(ks[17], (N_BUCKETS, A_HEADS), 0.3)
    norm_out_g = 1.0 + nrm(ks[18], (D_MODEL,), 0.02)
    return {'x': x, 'norm_in_g': norm_in_g, 'w_in': w_in, 'gate_b': gate_b, 'conv_w': conv_w, 'conv_b': conv_b,
            'wq_m': wq_m, 'wk_m': wk_m, 'wv_m': wv_m, 'w_if': w_if, 'b_if': b_if, 'head_norm_g': head_norm_g,
            'skip_m': skip_m, 'w_pa': w_pa, 'w_pb': w_pb, 'w_out': w_out, 'rel_bias': rel_bias, 'norm_out_g': norm_out_g}


def reference(x, norm_in_g, w_in, gate_b, conv_w, conv_b, wq_m, wk_m, wv_m, w_if, b_if, head_norm_g,
              skip_m, w_pa, w_pb, w_out, rel_bias, norm_out_g):
    B, S, _ = x.shape
    h = x
    for layer in range(DEPTH):
        xn = rmsnorm(h, norm_in_g[layer])
        proj = xn @ w_in[layer]
        q_a, k_a, v_a, z_a, x_m, z_m, o_m, gates = jnp.split(proj, SPLIT_POINTS, axis=-1)
        to_heads = lambda t: t.reshape(B, S, A_HEADS, A_HEAD_DIM)
        y_a = dilated_attention(to_heads(q_a), to_heads(k_a), to_heads(v_a), rel_bias)
        y_a = (y_a * jax.nn.silu(z_a)) @ w_pa[layer]
        y_m = mlstm_branch(x_m, z_m, o_m, conv_w[layer], conv_b[layer], wq_m[layer], wk_m[layer], wv_m[layer],
                           w_if[layer], b_if[layer], head_norm_g[layer], skip_m[layer]) @ w_pb[layer]
        g = jax.nn.sigmoid(gates.astype(jnp.float32) + gate_b[layer])
        g_a, g_m = jnp.split(g, 2, axis=-1)
        merged = g_a * y_a + g_m * y_m
        h = h + (merged @ w_out[layer]).astype(h.dtype)
    return rmsnorm(h, norm_out_g)
```

```python
import math
import os
from contextlib import ExitStack

import numpy as np
import concourse.bass as bass
import concourse.mybir as mybir
from concourse.bass_utils import run_bass_kernel_spmd

F32 = mybir.dt.float32
BF16 = mybir.dt.bfloat16
ALU = mybir.AluOpType
AF = mybir.ActivationFunctionType
AX = mybir.AxisListType

S = 4096
D = 1024
NT = S // 128
N_IN = 12288
EPS = 1e-6
NCORES = 8


class Sched:
    ENG = ["pe", "act", "dve", "pool", "sp"]
    NDMA = {"sp": 24, "pool": 12, "act": 8}

    def __init__(self, nc, stack):
        self.nc = nc
        self.e = dict(pe=nc.tensor, act=nc.scalar, dve=nc.vector, pool=nc.gpsimd, sp=nc.sync)
        self.prog = {k: [] for k in self.ENG}
        self.lastw = {}
        self.readers = {}
        self.dma_k = {q: 0 for q in self.NDMA}
        self.dma_recent = {q: [] for q in self.NDMA}
        self.extra = {k: set() for k in self.ENG}
        self.force = set()
        self._collect = None
        self.sem = {k: stack.enter_context(nc.semaphore("s_" + k)) for k in self.ENG}
        self.dsem = {q: [stack.enter_context(nc.semaphore(f"d_{q}{i}")) for i in range(n)]
                     for q, n in self.NDMA.items()}
        self.emitted = {k: 0 for k in self.ENG}
        self.cum = {k: 0 for k in self.ENG}
        self.cumat = {k: {} for k in self.ENG}
        self.seen = {k: {x: -1 for x in self.ENG} for k in self.ENG}
        self.seen_dma = {k: set() for k in self.ENG}
        self.know = {k: [] for k in self.ENG}

    def collect(self, stage_fn, *args):
        self._collect = []
        stage_fn(*args)
        lst = self._collect
        self._collect = None
        return lst

    def emit_interleaved(self, lists):
        for it in self.merge_lists(lists):
            self.op(*it)

    def merge_lists(self, lists):
        items = []
        for li, l in enumerate(lists):
            n = len(l)
            prev = None
            ppos = 0.0
            for j, it in enumerate(l):
                pos = (j + 0.5) / n
                if it[0] == "pe" and prev is not None and prev[0] == "pe" and prev[3] == it[3]:
                    pos = ppos
                items.append((pos, li, j, it))
                prev = it
                ppos = pos
        items.sort(key=lambda t: (t[0], t[1]))
        return [it for _, _, _, it in items]

    def op(self, eng, fn, reads=(), writes=(), dma=False):
        if getattr(self, "_collect", None) is not None:
            self._collect.append((eng, fn, tuple(reads), tuple(writes), dma))
            return None
        deps = set(self.extra[eng])
        self.extra[eng] = set()
        for r in reads:
            w = self.lastw.get(r)
            if w is not None:
                deps.add(w)
        for w_ in writes:
            w = self.lastw.get(w_)
            if w is not None:
                deps.add(w)
            for rd in self.readers.get(w_, ()):
                deps.add(rd)
        idx = len(self.prog[eng])
        if dma:
            k = self.dma_k[eng]
            self.dma_k[eng] += 1
            tok = ("dma", eng, k)
            n = self.NDMA[eng]
            if k >= n:
                deps.add(("dma", eng, k - n))
            self.dma_recent[eng].append(tok)
        else:
            tok = ("eng", eng, idx)
        self.prog[eng].append(dict(fn=fn, deps=deps, tok=tok, dma=dma))
        for r in reads:
            self.readers.setdefault(r, []).append(tok)
        for w_ in writes:
            self.lastw[w_] = tok
            self.readers[w_] = []
        return tok

    def simulate(self):
        vals = {}
        ptr = {k: 0 for k in self.ENG}
        progress = True
        while progress:
            progress = False
            for k in self.ENG:
                st = self.stream[k]
                while ptr[k] < len(st):
                    wl, inc = st[ptr[k]]
                    if all(vals.get(sk, 0) >= v for sk, v in wl):
                        if inc is not None:
                            vals[inc[0]] = vals.get(inc[0], 0) + inc[1]
                        ptr[k] += 1
                        progress = True
                    else:
                        break
        stuck = {k: (ptr[k], len(self.stream[k])) for k in self.ENG if ptr[k] < len(self.stream[k])}
        for k, (p, n) in stuck.items():
            wl, inc = self.stream[k][p]
            print("STUCK", k, p, n, [(sk, v, vals.get(sk, 0)) for sk, v in wl])
        return not stuck

    def dma(self, q, out, in_, reads=(), writes=()):
        return self.op(q, lambda e: e.dma_start(out=out, in_=in_), reads, writes, dma=True)

    def barrier(self):
        toks = set()
        for q in self.NDMA:
            if not self.dma_recent[q]:
                continue
            recent = self.dma_recent[q][-self.NDMA[q]:]
            self.dma_recent[q] = []
            sem = self.sem[q]
            self.extra[q] |= set(recent)
            for i in range(len(self.prog[q]) - 1, -1, -1):
                ins = self.prog[q][i]
                if not ins["dma"] and not ins.get("selfinc"):
                    self.extra[q].add(ins["tok"])
                    break
            t = self.op(q, lambda e, sem=sem: e.sem_inc(sem, 1), (), ())
            self.prog[q][-1]["selfinc"] = True
            toks.add(t)
        for k in self.ENG:
            if k in ("sp",):
                continue
            for i in range(len(self.prog[k]) - 1, -1, -1):
                ins = self.prog[k][i]
                if not ins["dma"] and not ins.get("selfinc"):
                    toks.add(ins["tok"])
                    break
        for k in self.ENG:
            self.extra[k] |= toks
        self.force |= {t for t in toks if t[0] == "eng"}

    def flush(self):
        start = dict(self.emitted)
        waits = {k: {} for k in self.ENG}
        marked = {k: set() for k in self.ENG}
        ptr = dict(start)
        n_total = {k: len(self.prog[k]) for k in self.ENG}
        for t in self.force:
            if t[2] >= start[t[1]]:
                marked[t[1]].add(t[2])
        self.force = set()
        progress = True
        while progress:
            progress = False
            for k in self.ENG:
                while ptr[k] < n_total[k]:
                    i = ptr[k]
                    ins = self.prog[k][i]
                    ok = True
                    for d in ins["deps"]:
                        if d[0] == "eng" and d[1] != k and d[2] >= ptr[d[1]]:
                            ok = False
                            break
                    if not ok:
                        break
                    need_eng = {}
                    need_dma = []
                    for d in ins["deps"]:
                        if d[0] == "eng":
                            x, j = d[1], d[2]
                            if x == k and k in ("pe", "sp"):
                                continue
                            if x == k and j >= i:
                                continue
                            if self.seen[k][x] >= j:
                                continue
                            if need_eng.get(x, -1) < j:
                                need_eng[x] = j
                        else:
                            if d in self.seen_dma[k]:
                                continue
                            need_dma.append(d)
                    for x, j in need_eng.items():
                        marked[x].add(j)
                        kn = self.know[x][j]
                        for y, v in kn.items():
                            if self.seen[k][y] < v:
                                self.seen[k][y] = v
                        if self.seen[k][x] < j:
                            self.seen[k][x] = j
                    for d in need_dma:
                        self.seen_dma[k].add(d)
                    waits[k][i] = (need_eng, need_dma)
                    kn = dict(self.seen[k])
                    if not ins["dma"]:
                        kn[k] = max(kn[k], i - 1)
                    self.know[k].append(kn)
                    ptr[k] += 1
                    progress = True
        for k in self.ENG:
            assert ptr[k] == n_total[k], f"deadlock in dependency graph on {k} at {ptr[k]}"
        for k in self.ENG:
            c = self.cum[k]
            for i in range(start[k], n_total[k]):
                ins = self.prog[k][i]
                if ins.get("selfinc") or (i in marked[k] and not ins["dma"]):
                    c += 1
                    self.cumat[k][i] = c
            self.cum[k] = c
        sched = self

        if not hasattr(self, "stream"):
            self.stream = {k: [] for k in self.ENG}

        def run(k, eng):
            for i in range(start[k], n_total[k]):
                ins = sched.prog[k][i]
                need_eng, need_dma = waits[k][i]
                wl = []
                for x, j in need_eng.items():
                    eng.wait_ge(sched.sem[x], sched.cumat[x][j])
                    wl.append((("e", x), sched.cumat[x][j]))
                for d in need_dma:
                    _, q, kk = d
                    n = sched.NDMA[q]
                    eng.wait_ge(sched.dsem[q][kk % n], 16 * (kk // n + 1))
                    wl.append((("d", q, kk % n), 16 * (kk // n + 1)))
                r = ins["fn"](eng)
                inc = None
                if ins["dma"]:
                    _, q, kk = ins["tok"]
                    r.then_inc(sched.dsem[q][kk % sched.NDMA[q]], 16)
                    inc = (("d", q, kk % sched.NDMA[q]), 16)
                elif ins.get("selfinc"):
                    inc = (("e", k), 1)
                elif i in marked[k]:
                    r.then_inc(sched.sem[k], 1)
                    inc = (("e", k), 1)
                sched.stream[k].append((wl, inc))

        with self.nc.Block() as block:
            @block.tensor
            def _(eng):
                run("pe", eng)

            @block.scalar
            def _(eng):
                run("act", eng)

            @block.vector
            def _(eng):
                run("dve", eng)

            @block.gpsimd
            def _(eng):
                run("pool", eng)

            @block.sync
            def _(eng):
                run("sp", eng)
        for k in self.ENG:
            self.emitted[k] = n_total[k]


ALL_PHASES = ("p1", "p2", "ma", "mb", "mc", "p3")
MERGE_MA = os.environ.get("MERGE_MA", "1") == "1"


def build(debug_outs=(), phases=ALL_PHASES):
    nc = bass.Bass("TRN2", target_bir_lowering=False)
    dbg = set(debug_outs)

    def dram_in(name, shape, dt=F32):
        return nc.dram_tensor(name, list(shape), dt, kind="ExternalInput").ap()

    def scratch(name, shape, dt):
        kind = "ExternalOutput" if name in dbg else "Internal"
        return nc.dram_tensor(name, list(shape), dt, kind=kind).ap()

    x_d = dram_in("x", [S, D])
    w_in_d = dram_in("w_in", [D, N_IN])
    gin_d = dram_in("gin_bc", [128, D])
    gateb_d = dram_in("gate_b_fm", [128, 16])
    ident_d = dram_in("ident", [128, 128])
    out_d = nc.dram_tensor("out", [S, D], F32, kind="ExternalOutput").ap()

    bias_d = dram_in("bias_tab", [3, 128, 16, 2, 128])
    mask_d = dram_in("mask_tab", [128, 2, 128])
    att_d = [scratch(f"att{p}", [S, 16, 65], F32) for p in range(3)]
    cw_d = dram_in("cw", [128, 16, 4])
    cb_d = dram_in("cb", [128, 16])
    bif_d = dram_in("bif", [8, 1])
    wif_d = dram_in("wif", [128, 48, 8])
    bdq_d = dram_in("bdq", [128, 16, 128])
    bdk_d = dram_in("bdk", [128, 16, 128])
    bdv_d = dram_in("bdv", [128, 16, 128])
    sel_d = dram_in("sel127", [128, 128])
    wpa_d = dram_in("w_pa", [1024, 1024])
    wpb_d = dram_in("w_pb", [2048, 1024])
    wout_d = dram_in("w_out", [1024, 1024])
    gout_d = dram_in("gout_bc", [128, 1024])
    skip_d = dram_in("skip_fm", [128, 16])
    hng_d = dram_in("hng_bc", [128, 2048])
    tri_d = dram_in("tri", [128, 128])
    sxT_d = scratch("sxT", [2048, S], F32)
    hzT_d = scratch("hzT", [2048, S], BF16)
    xcbT_d = scratch("xcbT", [2048, S], BF16)
    xmbT_d = scratch("xmbT", [2048, S], BF16)
    qmT_d = scratch("qmT", [2048, S], BF16)
    ksT_d = scratch("ksT", [2048, S], BF16)
    gpre_d = scratch("gpre", [8, S], F32)
    gtm_d = scratch("gtm", [128, 32, 16], F32)
    glast_d = scratch("glast", [128, 32, 16], F32)
    qaT_d = scratch("qaT", [D, S], BF16)
    kaT_d = scratch("kaT", [D, S], BF16)
    va_d = scratch("va1", [S, 16, 65], BF16)
    za_d = scratch("za", [S, D], F32)
    xmT_d = scratch("xmT", [2048, S], F32)
    zmT_d = scratch("zmT", [2048, S], F32)
    om_d = scratch("om", [S, 2048], F32)
    gT_d = scratch("gT", [2048, S], F32)

    with ExitStack() as top:
        k = Sched(nc, top)
        if "p1" in phases:
            with ExitStack() as ph:
                def sb(name, shape, dt):
                    return ph.enter_context(nc.sbuf_tensor("sb_" + name, list(shape), dt))

                def ps(name, shape, dt=F32):
                    return ph.enter_context(nc.psum_tensor("ps_" + name, list(shape), dt))

                xnT = sb("xnT", [128, 8, S], BF16)
                xt = [sb(f"xt{i}", [128, D], F32) for i in range(2)]
                sq = sb("sq", [128, D], F32)
                xnb = [sb(f"xnb{i}", [128, D], BF16) for i in range(2)]
                gin = sb("gin", [128, D], F32)
                gateb = sb("gateb", [128, 16], F32)
                ident = sb("ident", [128, 128], BF16)
                identf = sb("identf", [128, 128], F32)
                ss = sb("ss", [128, NT], F32)
                rstd = sb("rstd", [128, NT], F32)
                wbf = [sb(f"wbf{i}", [128, 8, 512], BF16) for i in range(2)]
                stg = [sb(f"stg{i}", [128, 4096], F32) for i in range(2)]
                psT = [ps(f"psT{i}", [128, 1024], BF16) for i in range(2)]
                psM = [ps(f"psM{i}", [128, 512], F32) for i in range(4)]

                k.dma("sp", gin[:], gin_d, writes=["gin"])
                k.dma("sp", gateb[:], gateb_d, writes=["gateb"])
                k.dma("sp", identf[:], ident_d, writes=["identf"])
                k.op("dve", lambda e: e.tensor_copy(out=ident[:], in_=identf[:]),
                     reads=["identf"], writes=["ident"])

                for t in range(NT):
                    b = t % 2
                    k.dma("sp", xt[b][:], x_d[t * 128:(t + 1) * 128, :], writes=[f"xt{b}"])
                    k.op("act", lambda e, b=b, t=t: e.activation(
                        out=sq[:], in_=xt[b][:], func=AF.Square, accum_out=ss[:, t:t + 1]),
                        reads=[f"xt{b}"], writes=["sq", f"ss{t}"])
                    k.op("act", lambda e, t=t: e.activation(
                        out=rstd[:, t:t + 1], in_=ss[:, t:t + 1], func=AF.Sqrt, scale=1.0 / D, bias=EPS),
                        reads=[f"ss{t}"], writes=[f"rstd{t}"])
                    k.op("dve", lambda e, t=t: e.reciprocal(out=rstd[:, t:t + 1], in_=rstd[:, t:t + 1]),
                         reads=[f"rstd{t}"], writes=[f"rstd{t}"])
                    k.op("dve", lambda e, b=b, t=t: e.scalar_tensor_tensor(
                        out=xnb[b][:], in0=xt[b][:], scalar=rstd[:, t:t + 1], in1=gin[:],
                        op0=ALU.mult, op1=ALU.mult),
                        reads=[f"xt{b}", f"rstd{t}", "gin"], writes=[f"xnb{b}"])
                    for c in range(8):
                        k.op("pe", lambda e, b=b, c=c: e.transpose(
                            out=psT[b][:, c * 128:(c + 1) * 128], in_=xnb[b][:, c * 128:(c + 1) * 128],
                            identity=ident[:]),
                            reads=[f"xnb{b}", "ident"], writes=[f"psT{b}"])
                    evac = "act" if t % 2 == 0 else "dve"
                    if evac == "act":
                        k.op("act", lambda e, b=b, t=t: e.copy(
                            out=xnT[:, :, t * 128:(t + 1) * 128],
                            in_=psT[b][:].rearrange("p (c t) -> p c t", c=8)),
                            reads=[f"psT{b}"], writes=[f"xnT{t}"])
                    else:
                        k.op("dve", lambda e, b=b, t=t: e.tensor_copy(
                            out=xnT[:, :, t * 128:(t + 1) * 128],
                            in_=psT[b][:].rearrange("p (c t) -> p c t", c=8)),
                            reads=[f"psT{b}"], writes=[f"xnT{t}"])
                xnT_all = [f"xnT{t}" for t in range(NT)]

                cnt = {"psi": 0, "stgi": 0}

                def blk_ops(cb, wb):
                    psi = cnt["psi"]
                    stgi = cnt["stgi"]
                    col0 = cb * 512
                    k.dma("pool", wbf[wb][:],
                          w_in_d[:, col0:col0 + 512].rearrange("(kc p) c -> p kc c", p=128),
                          writes=[f"wbf{wb}"])
                    if col0 < 1024:
                        kind = ("fm", qaT_d, col0, BF16, None)
                    elif col0 < 2048:
                        kind = ("fm", kaT_d, col0 - 1024, BF16, None)
                    elif col0 < 3072:
                        kind = ("va", va_d, col0 - 2048, BF16, None)
                    elif col0 < 4096:
                        kind = ("tm", za_d, col0 - 3072, F32, AF.Silu)
                    elif col0 < 6144:
                        kind = ("fm", xmT_d, col0 - 4096, F32, None)
                    elif col0 < 8192:
                        kind = ("fm", zmT_d, col0 - 6144, F32, AF.Silu)
                    elif col0 < 10240:
                        kind = ("tm", om_d, col0 - 8192, F32, AF.Sigmoid)
                    else:
                        kind = ("fm", gT_d, col0 - 10240, F32, AF.Sigmoid)
                    lay, dst, c0, odt, func = kind
                    if lay == "fm":
                        for cc in range(4):
                            f0 = c0 + cc * 128
                            sg = stgi % 2
                            stgi += 1
                            st = stg[sg]
                            if odt == BF16:
                                stv = st[:].bitcast(BF16)[:, 0:S]
                            else:
                                stv = st[:]
                            for g in range(8):
                                p = psi % 4
                                psi += 1
                                for kc in range(8):
                                    k.op("pe", lambda e, p=p, wb=wb, kc=kc, cc=cc, g=g: e.matmul(
                                        psM[p][:], lhsT=wbf[wb][:, kc, cc * 128:(cc + 1) * 128],
                                        rhs=xnT[:, kc, g * 512:(g + 1) * 512],
                                        start=(kc == 0), stop=(kc == 7)),
                                        reads=[f"wbf{wb}"] + xnT_all[g * 4:(g + 1) * 4], writes=[f"psM{p}"])
                                osl = stv[:, g * 512:(g + 1) * 512]
                                if func is None and (g % 2 == 0 or not MERGE_MA):
                                    k.op("dve", lambda e, p=p, osl=osl: e.tensor_copy(out=osl, in_=psM[p][:]),
                                         reads=[f"psM{p}"], writes=[f"stg{sg}_{g}"])
                                elif func is None:
                                    k.op("act", lambda e, p=p, osl=osl: e.copy(out=osl, in_=psM[p][:]),
                                         reads=[f"psM{p}"], writes=[f"stg{sg}_{g}"])
                                elif dst is gT_d:
                                    ch = f0 // 128
                                    k.op("act", lambda e, p=p, osl=osl, ch=ch: e.activation(
                                        out=osl, in_=psM[p][:], func=AF.Sigmoid, bias=gateb[:, ch:ch + 1]),
                                        reads=[f"psM{p}", "gateb"], writes=[f"stg{sg}_{g}"])
                                else:
                                    k.op("act", lambda e, p=p, osl=osl, func=func: e.activation(
                                        out=osl, in_=psM[p][:], func=func),
                                        reads=[f"psM{p}"], writes=[f"stg{sg}_{g}"])
                            k.dma("sp", dst[f0:f0 + 128, :], stv, reads=[f"stg{sg}_{g_}" for g_ in range(8)],
                                  writes=([f"xmT_d{f0 // 128}"] if dst is xmT_d else []))
                    else:
                        for tg in range(8):
                            sg = stgi % 2
                            stgi += 1
                            st = stg[sg]
                            if lay == "va":
                                stv = st[:].bitcast(BF16)[:, 0:4 * 8 * 65].rearrange(
                                    "p (t h c) -> p t h c", t=4, h=8)
                                if tg < 2 and cb == 4:
                                    pass
                                k.op("pool", lambda e, stv=stv: e.memset(stv[:, :, :, 64:65], 1.0),
                                     writes=[f"stg{sg}_{g_}" for g_ in range(8)])
                            else:
                                stv = st[:, 0:2048].rearrange("p (t c) -> p t c", t=4)
                            for tt in range(4):
                                t = tg * 4 + tt
                                p = psi % 4
                                psi += 1
                                for kc in range(8):
                                    k.op("pe", lambda e, p=p, wb=wb, kc=kc, t=t: e.matmul(
                                        psM[p][:], lhsT=xnT[:, kc, t * 128:(t + 1) * 128],
                                        rhs=wbf[wb][:, kc, :], start=(kc == 0), stop=(kc == 7)),
                                        reads=[f"wbf{wb}", f"xnT{t}"], writes=[f"psM{p}"])
                                if lay == "va":
                                    k.op("dve", lambda e, p=p, stv=stv, tt=tt: e.tensor_copy(
                                        out=stv[:, tt, :, 0:64],
                                        in_=psM[p][:].rearrange("p (h c) -> p h c", h=8)),
                                        reads=[f"psM{p}"], writes=[f"stg{sg}_{tt}"])
                                else:
                                    k.op("act", lambda e, p=p, stv=stv, tt=tt, func=func: e.activation(
                                        out=stv[:, tt, :], in_=psM[p][:], func=func),
                                        reads=[f"psM{p}"], writes=[f"stg{sg}_{tt}"])
                            r0 = tg * 512
                            if lay == "va":
                                h0 = c0 // 64
                                k.dma("sp", dst[r0:r0 + 512, h0:h0 + 8, :].rearrange("(t p) h c -> p t h c", p=128),
                                      stv, reads=[f"stg{sg}_{g_}" for g_ in range(8)])
                            else:
                                k.dma("sp", dst[r0:r0 + 512, c0:c0 + 512].rearrange("(t p) c -> p t c", p=128),
                                      stv, reads=[f"stg{sg}_{g_}" for g_ in range(8)])
                    cnt["psi"] = psi
                    cnt["stgi"] = stgi

                ORDER = [8, 9, 10, 11, 0, 1, 2, 3, 4, 5, 6, 7, 12, 13, 14, 15, 16, 17, 18, 19, 20, 21, 22, 23]
                if not MERGE_MA:
                    for i_, cb in enumerate(range(24)):
                        blk_ops(cb, i_ % 2)
                else:
                    cw = sb("ma_cw", [128, 16, 4], F32)
                    cb = sb("ma_cb", [128, 16], F32)
                    bif = sb("ma_bif", [8, 1], F32)
                    bdf = sb("ma_bdf", [128, 16, 128], F32)
                    bd = [sb(f"ma_bd{i}", [128, 16, 128], BF16) for i in range(3)]
                    wiff = sb("ma_wiff", [128, 48, 8], F32)
                    wif = sb("ma_wif", [128, 48, 8], BF16)
                    xm = [sb(f"ma_xm{i}", [128, 515], F32) for i in range(2)]
                    acc = [sb(f"ma_acc{i}", [128, 512], F32) for i in range(2)]
                    tmpc = [sb(f"ma_tmpc{i}", [128, 512], F32) for i in range(2)]
                    xcf = [sb(f"ma_xcf{i}", [128, 512], F32) for i in range(2)]
                    xcb = [sb(f"ma_xcb{i}", [128, 512], BF16) for i in range(2)]
                    sxf = [sb(f"ma_sxf{i}", [128, 512], F32) for i in range(2)]
                    skp = sb("ma_skp", [128, 16], F32)
                    xmb = [sb(f"ma_xmb{i}", [128, 512], BF16) for i in range(2)]
                    qTb = [sb(f"ma_qTb{i}", [128, 512], BF16) for i in range(3)]
                    kTb = [sb(f"ma_kTb{i}", [128, 512], BF16) for i in range(3)]
                    ksb = [sb(f"ma_ksb{i}", [128, 512], BF16) for i in range(2)]
                    vTb = [sb(f"ma_vTb{i}", [128, 512], BF16) for i in range(3)]
                    gsb = [sb(f"ma_gsb{i}", [8, 512], F32) for i in range(2)]
                    psq = [psT[0][:].bitcast(F32)] * 2
                    psk = [psT[1][:].bitcast(F32)] * 2
                    psv_t = ps("mav", [128, 512])
                    psv = [psv_t[:]] * 2
                    psg_t = ps("mag", [128, 512])
                    psg = [psg_t[:]] * 2

                    k.dma("sp", cw[:], cw_d, writes=["cw"])
                    k.dma("sp", cb[:], cb_d, writes=["cb"])
                    k.dma("sp", skp[:], skip_d, writes=["skp"])
                    k.dma("sp", bif[:], bif_d, writes=["bif"])
                    k.dma("sp", wiff[:], wif_d, writes=["wiff"])
                    k.op("dve", lambda e: e.tensor_copy(out=wif[:], in_=wiff[:]), reads=["wiff"], writes=["wif"])
                    for i, bdd in enumerate((bdq_d, bdk_d, bdv_d)):
                        k.dma("sp", bdf[:], bdd, writes=["bdf"])
                        k.op("dve", lambda e, i=i: e.tensor_copy(out=bd[i][:], in_=bdf[:]),
                             reads=["bdf"], writes=[f"bd{i}"])
                    KS = float(512 ** -0.5)

                    def load_xm(it_):
                        g_, c_ = it_ // 16, it_ % 16
                        b_ = it_ % 2
                        rows_ = slice(c_ * 128, (c_ + 1) * 128)
                        if g_ == 0:
                            k.op("pool", lambda e: e.memset(xm[b_][:, 0:3], 0.0), writes=[f"xm{b_}"])
                            k.dma("sp", xm[b_][:, 3:515], xmT_d[rows_, 0:512], reads=[f"xmT_d{c_}"], writes=[f"xm{b_}"])
                        else:
                            k.dma("sp", xm[b_][:], xmT_d[rows_, g_ * 512 - 3:g_ * 512 + 512], reads=[f"xmT_d{c_}"], writes=[f"xm{b_}"])


                    def ma_x(it):
                        g, c = it // 16, it % 16
                        b = it % 2
                        t0 = g * 512
                        rows = slice(c * 128, (c + 1) * 128)
                        k.op("act", lambda e: e.activation(
                            out=acc[b][:], in_=xm[b][:, 0:512], func=AF.Identity, scale=cw[:, c, 0:1]),
                            reads=[f"xm{b}", "cw"], writes=[f"acc{b}"])
                        for tap in range(1, 4):
                            k.op("dve", lambda e, tap=tap: e.scalar_tensor_tensor(
                                out=acc[b][:], in0=xm[b][:, tap:tap + 512], scalar=cw[:, c, tap:tap + 1],
                                in1=acc[b][:], op0=ALU.mult, op1=ALU.add),
                                reads=[f"xm{b}", "cw", f"acc{b}"], writes=[f"acc{b}"])
                        k.op("act", lambda e: e.activation(
                            out=xcf[b][:], in_=acc[b][:], func=AF.Silu, bias=cb[:, c:c + 1]),
                            reads=[f"acc{b}", "cb"], writes=[f"xcf{b}"])
                        k.op("dve", lambda e: e.tensor_copy(out=xcb[b][:], in_=xcf[b][:]),
                             reads=[f"xcf{b}"], writes=[f"xcb{b}"])
                        k.op("dve", lambda e: e.tensor_copy(out=xmb[b][:], in_=xm[b][:, 3:515]),
                             reads=[f"xm{b}"], writes=[f"xmb{b}"])
                        k.op("dve", lambda e: e.tensor_scalar(
                            out=sxf[b][:], in0=xcf[b][:], scalar1=skp[:, c:c + 1], scalar2=None, op0=ALU.mult),
                            reads=[f"xcf{b}", "skp"], writes=[f"sxf{b}"])
                        k.dma("sp", sxT_d[rows, t0:t0 + 512], sxf[b][:], reads=[f"sxf{b}"])
                        k.dma("sp", xcbT_d[rows, t0:t0 + 512], xcb[b][:], reads=[f"xcb{b}"])
                        k.dma("sp", xmbT_d[rows, t0:t0 + 512], xmb[b][:], reads=[f"xmb{b}"])

                    def ma_y(it):
                        g, c = it // 16, it % 16
                        b = it % 2
                        b3 = it % 3
                        t0 = g * 512
                        rows = slice(c * 128, (c + 1) * 128)
                        gp = psg[g % 2]
                        gpn = "psgm"
                        k.op("pe", lambda e: e.matmul(psq[b], lhsT=bd[0][:, c, :], rhs=xcb[b][:], start=True, stop=True),
                             reads=["bd0", f"xcb{b}"], writes=["psT0"])
                        k.op("pe", lambda e: e.matmul(psk[b], lhsT=bd[1][:, c, :], rhs=xcb[b][:], start=True, stop=True),
                             reads=["bd1", f"xcb{b}"], writes=["psT1"])
                        k.op("pe", lambda e: e.matmul(psv[b], lhsT=bd[2][:, c, :], rhs=xmb[b][:], start=True, stop=True),
                             reads=["bd2", f"xmb{b}"], writes=["psvm"])
                        k.op("dve", lambda e: e.tensor_copy(out=qTb[b3][:], in_=psq[b]),
                             reads=["psT0"], writes=[f"qTb{b3}"])
                        k.op("act", lambda e: e.copy(out=kTb[b3][:], in_=psk[b]),
                             reads=["psT1"], writes=[f"kTb{b3}"])
                        k.op("act", lambda e: e.activation(out=ksb[b][:], in_=psk[b], func=AF.Identity, scale=KS),
                             reads=["psT1"], writes=[f"ksb{b}"])
                        k.op("act", lambda e: e.copy(out=vTb[b3][:], in_=psv[b]),
                             reads=["psvm"], writes=[f"vTb{b3}"])
                        k.dma("sp", qmT_d[rows, t0:t0 + 512], qTb[b3][:], reads=[f"qTb{b3}"])
                        k.dma("sp", ksT_d[rows, t0:t0 + 512], ksb[b][:], reads=[f"ksb{b}"])

                    def ma_z(it):
                        g, c = it // 16, it % 16
                        b3 = it % 3
                        t0 = g * 512
                        gp = psg[g % 2]
                        gpn = "psgm"
                        for j, (src, sn) in enumerate(((qTb, "qTb"), (kTb, "kTb"), (vTb, "vTb"))):
                            k.op("pe", lambda e, j=j, src=src: e.matmul(
                                gp[0:8, :], lhsT=wif[:, j * 16 + c, :], rhs=src[b3][:],
                                start=(c == 0 and j == 0), stop=(c == 15 and j == 2)),
                                reads=["wif", f"{sn}{b3}"], writes=[gpn])
                        if c == 15:
                            gs = gsb[g % 2]
                            k.op("act", lambda e: e.activation(
                                out=gs[:], in_=gp[0:8, :], func=AF.Identity, bias=bif[:, 0:1]),
                                reads=[gpn, "bif"], writes=[f"gsb{g % 2}"])
                            k.dma("sp", gpre_d[:, t0:t0 + 512], gs[:], reads=[f"gsb{g % 2}"])

                    def ma_iter_list(it):
                        ls = []
                        if it < 128:
                            def x_with_load(it=it):
                                if it + 1 < 128:
                                    load_xm(it + 1)
                                ma_x(it)
                            ls.append(k.collect(x_with_load))
                        if 1 <= it <= 128:
                            ls.append(k.collect(ma_y, it - 1))
                        if it >= 2:
                            ls.append(k.collect(ma_z, it - 2))
                        return k.merge_lists(ls)

                    for i_ in range(4):
                        blk_ops(ORDER[i_], i_ % 2)
                    load_xm(0)
                    it_next = 0
                    NOV = 12
                    for i_ in range(4, 4 + NOV):
                        n_it = (130 - it_next + (4 + NOV - i_) - 1) // (4 + NOV - i_)
                        ma_l = []
                        for it in range(it_next, it_next + n_it):
                            ma_l += ma_iter_list(it)
                        it_next += n_it
                        bl = k.collect(blk_ops, ORDER[i_], i_ % 2)
                        k.emit_interleaved([bl, ma_l])
                    assert it_next == 130
                    for i_ in range(4 + NOV, 24):
                        blk_ops(ORDER[i_], i_ % 2)
                k.barrier()
                k.flush()
        if "p2" in phases:
            with ExitStack() as ph:
                def sb(name, shape, dt):
                    return ph.enter_context(nc.sbuf_tensor("sb2_" + name, list(shape), dt))

                def ps(name, shape, dt=F32):
                    return ph.enter_context(nc.psum_tensor("ps2_" + name, list(shape), dt))

                qT = [sb(f"qT{i}", [128, S], BF16) for i in range(2)]
                kTe = [sb(f"kTe{i}", [128, S], BF16) for i in range(2)]
                kTo = [sb(f"kTo{i}", [128, S], BF16) for i in range(2)]
                Eall = [sb(f"E{p}", [128, 16, 2, 128], F32) for p in range(3)]
                mask = sb("mask", [128, 2, 128], F32)
                vt = [sb(f"vt{i}", [128, 32, 2, 65], BF16) for i in range(2)]
                pex = [sb(f"pex{i}", [128, 2, 2, 128], F32) for i in range(4)]
                pt = [sb(f"pt{i}", [128, 2, 2, 128], BF16) for i in range(6)]
                ost = [sb(f"ost{i}", [128, 3, 130], F32) for i in range(3)]
                psL = [ps(f"psL{i}", [128, 2, 2, 128], F32) for i in range(4)]
                psO = [ps(f"psO{i}", [128, 512], F32) for i in range(2)]
                PATS = [(p, d) for p, d in enumerate((1, 4, 16))
                        if str(p) in os.environ.get('ATT_PATTERNS', '012')]
                NHP = int(os.environ.get("ATT_NHP", "8"))
                for i in range(2):
                    k.op("pool", lambda e, i=i: e.memset(kTe[i][64:128, :], 0.0), writes=[f"kTe{i}"])
                    k.op("pool", lambda e, i=i: e.memset(kTo[i][0:64, :], 0.0), writes=[f"kTo{i}"])
                k.dma("sp", mask[:], mask_d, writes=["mask"])

                def qk_load(hp):
                    sl = hp % 2
                    k.dma("sp", qT[sl][:], qaT_d[hp * 128:(hp + 1) * 128, :], writes=[f"qT{sl}"])
                    k.dma("sp", kTe[sl][0:64, :], kaT_d[hp * 128:hp * 128 + 64, :], writes=[f"kTe{sl}"])
                    k.dma("sp", kTo[sl][64:128, :], kaT_d[hp * 128 + 64:(hp + 1) * 128, :], writes=[f"kTo{sl}"])

                groups = [(hp, p, d) for hp in range(NHP) for (p, d) in PATS]

                def v_load(gi):
                    hp, p, d = groups[gi]
                    nb = 32 // d
                    sl = gi % 2
                    for r in range(d):
                        src = va_d[r:r + (S // d - 1) * d + 1:d, 2 * hp:2 * hp + 2, :].rearrange(
                            "(n j) h c -> j n h c", j=128)
                        k.dma("sp", vt[sl][:, r * nb:(r + 1) * nb, :, :], src, writes=[f"vt{sl}"])

                qk_load(0)
                v_load(0)
                for p, d in PATS:
                    E = Eall[p]
                    en = f"E{p}"
                    enh = [f"E{p}_{h}" for h in range(16)]
                    k.dma("sp", E[:], bias_d[p], writes=enh)
                    k.op("act", lambda e, E=E: e.activation(out=E[:], in_=E[:], func=AF.Exp),
                         reads=enh, writes=enh)
                    for h in range(16):
                        k.op("dve", lambda e, E=E, h=h: e.tensor_tensor(
                            out=E[:, h, :, :], in0=E[:, h, :, :], in1=mask[:], op=ALU.mult),
                            reads=[enh[h], "mask"], writes=[enh[h]])
                LAG = 3
                units = []
                for gi, (hp, p, d) in enumerate(groups):
                    nb = 32 // d
                    for bi in range(32):
                        units.append((gi, hp, p, d, bi, bi // nb, bi % nb))
                pend = []
                bc = 0

                def flush_o(bank, items):
                    ot = ost[bank % 3]
                    otn = f"ost{bank % 3}"
                    n_it = len(items)
                    pso = psO[bank % 2]
                    if True:
                        k.op("act", lambda e: e.copy(
                            out=ot[:, 0:n_it, :], in_=pso[:, 0:130 * n_it].rearrange("q (j c) -> q j c", j=n_it)),
                            reads=[f"psO{bank % 2}"], writes=[otn])
                    else:
                        k.op("dve", lambda e: e.tensor_copy(
                            out=ot[:, 0:n_it, :], in_=pso[:, 0:130 * n_it].rearrange("q (j c) -> q j c", j=n_it)),
                            reads=[f"psO{bank % 2}"], writes=[otn])
                    for j, (p_, hp_, rs) in enumerate(items):
                        k.dma("pool", att_d[p_][rs, 2 * hp_:2 * hp_ + 2, :],
                              ot[:, j, :].rearrange("q (h c) -> q h c", h=2), reads=[otn])

                for i in range(len(units) + LAG):
                    if i < len(units):
                        gi, hp, p, d, bi, r, n = units[i]
                        qs_ = hp % 2
                        q0 = r + d * 128 * n
                        qsl = slice(q0, q0 + 127 * d + 1, d)
                        ksl = [slice(q0 - 128 * d, q0 - 128 * d + 127 * d + 1, d) if n > 0 else None, qsl]
                        cs = [1] if n == 0 else [0, 1]
                        lsl = i % 4
                        L = psL[lsl]
                        ln = f"psL{lsl}"
                        for hh in range(2):
                            kz = kTe[qs_] if hh == 0 else kTo[qs_]
                            kzn = (f"kTe{qs_}" if hh == 0 else f"kTo{qs_}")
                            for c in cs:
                                k.op("pe", lambda e, L=L, c=c, hh=hh, kz=kz, qs_=qs_, ks=ksl[c], qs=qsl: e.matmul(
                                    L[:, hh, c, :], lhsT=kz[:, ks], rhs=qT[qs_][:, qs], start=True, stop=True),
                                    reads=[kzn, f"qT{qs_}"], writes=[ln])
                        c0 = cs[0]
                        px = pex[i % 4]
                        pxn = f"pex{i % 4}"
                        k.op("act", lambda e, L=L, px=px, c0=c0: e.activation(
                            out=px[:, :, c0:2, :], in_=L[:, :, c0:2, :], func=AF.Exp, scale=0.125),
                            reads=[ln], writes=[pxn])
                        ptt = pt[i % 6]
                        ptn = f"pt{i % 6}"
                        E = Eall[p]
                        k.op("dve", lambda e, px=px, ptt=ptt, E=E, hp=hp, c0=c0: e.tensor_tensor(
                            out=ptt[:, :, c0:2, :], in0=px[:, :, c0:2, :], in1=E[:, 2 * hp:2 * hp + 2, c0:2, :],
                            op=ALU.mult),
                            reads=[pxn, f"E{p}_{2 * hp}", f"E{p}_{2 * hp + 1}"], writes=[ptn])
                        if bi == LAG + 2:
                            if gi + 1 < len(groups):
                                v_load(gi + 1)
                                if groups[gi + 1][0] != hp:
                                    qk_load(groups[gi + 1][0])
                    j = i - LAG
                    if j >= 0:
                        gi, hp, p, d, bi, r, n = units[j]
                        cs = [1] if n == 0 else [0, 1]
                        ptt = pt[j % 6]
                        ptn = f"pt{j % 6}"
                        bank = bc // 3
                        j3 = bc % 3
                        bc += 1
                        vs = gi % 2
                        for hh in range(2):
                            O = psO[bank % 2][:, j3 * 130 + hh * 65:j3 * 130 + hh * 65 + 65]
                            for c in cs:
                                k.op("pe", lambda e, O=O, ptt=ptt, c=c, vs=vs, hh=hh, bi=bi, cs=cs: e.matmul(
                                    O, lhsT=ptt[:, hh, c, :], rhs=vt[vs][:, bi - 1 + c, hh, :],
                                    start=(c == cs[0]), stop=(c == 1)),
                                    reads=[ptn, f"vt{vs}"], writes=[f"psO{bank % 2}"])
                        st0 = r + d * 128 * n
                        pend.append((p, hp, slice(st0, st0 + 127 * d + 1, d)))
                        if j3 == 2 or j == len(units) - 1:
                            flush_o(bank, pend)
                            pend = []
                k.barrier()
                k.flush()
        if "ma" in phases and not MERGE_MA:
            with ExitStack() as ph:
                def sb(name, shape, dt):
                    return ph.enter_context(nc.sbuf_tensor("sb3_" + name, list(shape), dt))

                def ps(name, shape, dt=F32):
                    return ph.enter_context(nc.psum_tensor("ps3_" + name, list(shape), dt))

                cw = sb("cw", [128, 16, 4], F32)
                cb = sb("cb", [128, 16], F32)
                bif = sb("bif", [8, 1], F32)
                bdf = sb("bdf", [128, 16, 128], F32)
                bd = [sb(f"bd{i}", [128, 16, 128], BF16) for i in range(3)]
                wiff = sb("wiff", [128, 48, 8], F32)
                wif = sb("wif", [128, 48, 8], BF16)
                xm = [sb(f"xm{i}", [128, 515], F32) for i in range(2)]
                acc = [sb(f"acc{i}", [128, 512], F32) for i in range(2)]
                tmpc = [sb(f"tmpc{i}", [128, 512], F32) for i in range(2)]
                xcf = [sb(f"xcf{i}", [128, 512], F32) for i in range(2)]
                xcb = [sb(f"xcb{i}", [128, 512], BF16) for i in range(2)]
                sxf = [sb(f"sxf{i}", [128, 512], F32) for i in range(2)]
                skp = sb("skp", [128, 16], F32)
                xmb = [sb(f"xmb{i}", [128, 512], BF16) for i in range(2)]
                qTb = [sb(f"qTb{i}", [128, 512], BF16) for i in range(2)]
                kTb = [sb(f"kTb{i}", [128, 512], BF16) for i in range(2)]
                ksb = [sb(f"ksb{i}", [128, 512], BF16) for i in range(2)]
                vTb = [sb(f"vTb{i}", [128, 512], BF16) for i in range(2)]
                gsb = [sb(f"gsb{i}", [8, 512], F32) for i in range(2)]
                psq = [ps(f"q{i}", [128, 512]) for i in range(2)]
                psk = [ps(f"k{i}", [128, 512]) for i in range(2)]
                psv = [ps(f"v{i}", [128, 512]) for i in range(2)]
                psg = [ps(f"g{i}", [128, 512]) for i in range(2)]

                k.dma("sp", cw[:], cw_d, writes=["cw"])
                k.dma("sp", cb[:], cb_d, writes=["cb"])
                k.dma("sp", skp[:], skip_d, writes=["skp"])
                k.dma("sp", bif[:], bif_d, writes=["bif"])
                k.dma("sp", wiff[:], wif_d, writes=["wiff"])
                k.op("dve", lambda e: e.tensor_copy(out=wif[:], in_=wiff[:]), reads=["wiff"], writes=["wif"])
                for i, bdd in enumerate((bdq_d, bdk_d, bdv_d)):
                    k.dma("sp", bdf[:], bdd, writes=["bdf"])
                    k.op("dve", lambda e, i=i: e.tensor_copy(out=bd[i][:], in_=bdf[:]),
                         reads=["bdf"], writes=[f"bd{i}"])
                KS = float(512 ** -0.5)

                def load_xm(it_):
                    g_, c_ = it_ // 16, it_ % 16
                    b_ = it_ % 2
                    rows_ = slice(c_ * 128, (c_ + 1) * 128)
                    if g_ == 0:
                        k.op("pool", lambda e: e.memset(xm[b_][:, 0:3], 0.0), writes=[f"xm{b_}"])
                        k.dma("sp", xm[b_][:, 3:515], xmT_d[rows_, 0:512], writes=[f"xm{b_}"])
                    else:
                        k.dma("sp", xm[b_][:], xmT_d[rows_, g_ * 512 - 3:g_ * 512 + 512], writes=[f"xm{b_}"])

                load_xm(0)

                def ma_x(it):
                    g, c = it // 16, it % 16
                    b = it % 2
                    t0 = g * 512
                    rows = slice(c * 128, (c + 1) * 128)
                    k.op("act", lambda e: e.activation(
                        out=acc[b][:], in_=xm[b][:, 0:512], func=AF.Identity, scale=cw[:, c, 0:1]),
                        reads=[f"xm{b}", "cw"], writes=[f"acc{b}"])
                    for tap in range(1, 4):
                        k.op("dve", lambda e, tap=tap: e.scalar_tensor_tensor(
                            out=acc[b][:], in0=xm[b][:, tap:tap + 512], scalar=cw[:, c, tap:tap + 1],
                            in1=acc[b][:], op0=ALU.mult, op1=ALU.add),
                            reads=[f"xm{b}", "cw", f"acc{b}"], writes=[f"acc{b}"])
                    k.op("act", lambda e: e.activation(
                        out=xcf[b][:], in_=acc[b][:], func=AF.Silu, bias=cb[:, c:c + 1]),
                        reads=[f"acc{b}", "cb"], writes=[f"xcf{b}"])
                    k.op("dve", lambda e: e.tensor_copy(out=xcb[b][:], in_=xcf[b][:]),
                         reads=[f"xcf{b}"], writes=[f"xcb{b}"])
                    k.op("dve", lambda e: e.tensor_copy(out=xmb[b][:], in_=xm[b][:, 3:515]),
                         reads=[f"xm{b}"], writes=[f"xmb{b}"])
                    k.op("dve", lambda e: e.tensor_scalar(
                        out=sxf[b][:], in0=xcf[b][:], scalar1=skp[:, c:c + 1], scalar2=None, op0=ALU.mult),
                        reads=[f"xcf{b}", "skp"], writes=[f"sxf{b}"])
                    k.dma("sp", sxT_d[rows, t0:t0 + 512], sxf[b][:], reads=[f"sxf{b}"])
                    k.dma("sp", xcbT_d[rows, t0:t0 + 512], xcb[b][:], reads=[f"xcb{b}"])
                    k.dma("sp", xmbT_d[rows, t0:t0 + 512], xmb[b][:], reads=[f"xmb{b}"])

                def ma_y(it):
                    g, c = it // 16, it % 16
                    b = it % 2
                    t0 = g * 512
                    rows = slice(c * 128, (c + 1) * 128)
                    gp = psg[g % 2]
                    gpn = f"psg{g % 2}"
                    k.op("pe", lambda e: e.matmul(psq[b][:], lhsT=bd[0][:, c, :], rhs=xcb[b][:], start=True, stop=True),
                         reads=["bd0", f"xcb{b}"], writes=[f"psq{b}"])
                    k.op("pe", lambda e: e.matmul(psk[b][:], lhsT=bd[1][:, c, :], rhs=xcb[b][:], start=True, stop=True),
                         reads=["bd1", f"xcb{b}"], writes=[f"psk{b}"])
                    k.op("pe", lambda e: e.matmul(psv[b][:], lhsT=bd[2][:, c, :], rhs=xmb[b][:], start=True, stop=True),
                         reads=["bd2", f"xmb{b}"], writes=[f"psv{b}"])
                    k.op("dve", lambda e: e.tensor_copy(out=qTb[b][:], in_=psq[b][:]),
                         reads=[f"psq{b}"], writes=[f"qTb{b}"])
                    k.op("act", lambda e: e.copy(out=kTb[b][:], in_=psk[b][:]),
                         reads=[f"psk{b}"], writes=[f"kTb{b}"])
                    k.op("act", lambda e: e.activation(out=ksb[b][:], in_=psk[b][:], func=AF.Identity, scale=KS),
                         reads=[f"psk{b}"], writes=[f"ksb{b}"])
                    k.op("act", lambda e: e.copy(out=vTb[b][:], in_=psv[b][:]),
                         reads=[f"psv{b}"], writes=[f"vTb{b}"])
                    k.dma("sp", qmT_d[rows, t0:t0 + 512], qTb[b][:], reads=[f"qTb{b}"])
                    k.dma("sp", ksT_d[rows, t0:t0 + 512], ksb[b][:], reads=[f"ksb{b}"])
                    for j, (src, sn) in enumerate(((qTb, "qTb"), (kTb, "kTb"), (vTb, "vTb"))):
                        k.op("pe", lambda e, j=j, src=src: e.matmul(
                            gp[0:8, :], lhsT=wif[:, j * 16 + c, :], rhs=src[b][:],
                            start=(c == 0 and j == 0), stop=(c == 15 and j == 2)),
                            reads=["wif", f"{sn}{b}"], writes=[gpn])
                    if c == 15:
                        gs = gsb[g % 2]
                        k.op("act", lambda e: e.activation(
                            out=gs[:], in_=gp[0:8, :], func=AF.Identity, bias=bif[:, 0:1]),
                            reads=[gpn, "bif"], writes=[f"gsb{g % 2}"])
                        k.dma("sp", gpre_d[:, t0:t0 + 512], gs[:], reads=[f"gsb{g % 2}"])

                for it in range(129):
                    lists = []
                    if it < 128:
                        if it + 1 < 128:
                            load_xm(it + 1)
                        lists.append(k.collect(ma_x, it))
                    if it >= 1:
                        lists.append(k.collect(ma_y, it - 1))
                    k.emit_interleaved(lists)
                k.barrier()
                k.flush()
        if "mb" in phases:
            with ExitStack() as ph:
                def sb(name, shape, dt):
                    return ph.enter_context(nc.sbuf_tensor("sb4_" + name, list(shape), dt))

                def ps(name, shape, dt=F32):
                    return ph.enter_context(nc.psum_tensor("ps4_" + name, list(shape), dt))

                li = sb("li", [4, S], F32)
                fp = sb("fp", [4, S], F32)
                e1 = sb("e1", [4, S], F32)
                ll = sb("ll", [4, S], F32)
                ones = sb("ones", [4, S], F32)
                Bn = sb("Bn", [4, S], F32)
                cg = sb("cg", [4, S], F32)
                Mg = sb("Mg", [4, S], F32)
                qu = sb("qu", [4, S], F32)
                qw = sb("qw", [4, S], F32)
                qe = sb("qe", [4, S], F32)
                qs = sb("qs", [4, S], F32)
                rr = sb("rr", [4, 32], F32)
                nr = sb("nr", [4, 32], F32)
                nre = sb("nre", [4, 32], F32)
                idf = sb("idf", [128, 128], F32)
                sel = sb("sel", [128, 128], F32)
                G = sb("G", [128, 32, 16], F32)
                Gl = sb("Gl", [128, 32, 16], F32)
                psG = ps("G", [128, 32, 16])
                psGl = ps("Gl", [128, 512])
                k.dma("sp", li[:], gpre_d[0:4, :], writes=["li"])
                k.dma("sp", fp[:], gpre_d[4:8, :], writes=["fp"])
                k.dma("sp", idf[:], ident_d, writes=["idf"])
                k.dma("sp", sel[:], sel_d, writes=["sel"])
                k.op("act", lambda e: e.activation(out=e1[:], in_=fp[:], func=AF.Exp, scale=-1.0),
                     reads=["fp"], writes=["e1"])
                k.op("act", lambda e: e.activation(out=ll[:], in_=e1[:], func=AF.Ln, bias=1.0),
                     reads=["e1"], writes=["ll"])
                k.op("pool", lambda e: e.memset(ones[:], 1.0), writes=["ones"])
                k.op("dve", lambda e: e.tensor_tensor_scan(out=Bn[:], data0=ones[:], data1=ll[:], initial=0.0,
                                                           op0=ALU.mult, op1=ALU.add),
                     reads=["ones", "ll"], writes=["Bn"])
                k.op("dve", lambda e: e.tensor_tensor(out=cg[:], in0=li[:], in1=Bn[:], op=ALU.add),
                     reads=["li", "Bn"], writes=["cg"])
                k.op("dve", lambda e: e.tensor_tensor_scan(out=Mg[:], data0=cg[:], data1=cg[:], initial=0.0,
                                                           op0=ALU.max, op1=ALU.max),
                     reads=["cg"], writes=["Mg"])
                Mgv = Mg[:].rearrange("p (c j) -> p c j", j=128)
                k.op("pool", lambda e: e.memset(rr[:, 0:1], 0.0), writes=["rr"])
                k.op("dve", lambda e: e.tensor_copy(out=rr[:, 1:32], in_=Mgv[:, 0:31, 127]),
                     reads=["Mg"], writes=["rr"])
                k.op("dve", lambda e: e.tensor_scalar(out=nr[:], in0=rr[:], scalar1=-1.0, scalar2=None, op0=ALU.mult),
                     reads=["rr"], writes=["nr"])
                k.op("dve", lambda e: e.tensor_scalar(out=nre[:], in0=Mgv[:, :, 127], scalar1=-1.0, scalar2=None,
                                                      op0=ALU.mult),
                     reads=["Mg"], writes=["nre"])
                k.op("dve", lambda e: e.tensor_tensor(out=qe[:], in0=Bn[:], in1=Mg[:], op=ALU.subtract),
                     reads=["Bn", "Mg"], writes=["qe"])
                k.op("act", lambda e: e.activation(out=qe[:], in_=qe[:], func=AF.Exp), reads=["qe"], writes=["qe"])
                for c in range(32):
                    sl = slice(c * 128, (c + 1) * 128)
                    k.op("act", lambda e, sl=sl, c=c: e.activation(out=qu[:, sl], in_=cg[:, sl], func=AF.Exp,
                                                                  bias=nr[:, c:c + 1]),
                         reads=["cg", "nr"], writes=[f"qu{c}"])
                    k.op("act", lambda e, sl=sl, c=c: e.activation(out=qw[:, sl], in_=Mg[:, sl], func=AF.Exp,
                                                                  scale=-1.0, bias=rr[:, c:c + 1]),
                         reads=["Mg", "rr"], writes=[f"qw{c}"])
                    k.op("act", lambda e, sl=sl, c=c: e.activation(out=qs[:, sl], in_=cg[:, sl], func=AF.Exp,
                                                                  bias=nre[:, c:c + 1]),
                         reads=["cg", "nre"], writes=[f"qs{c}"])
                for c in range(32):
                    sl = slice(c * 128, (c + 1) * 128)
                    for i, (X, xn_) in enumerate(((qu, "qu"), (qw, "qw"), (qe, "qe"), (qs, "qs"))):
                        k.op("pe", lambda e, X=X, sl=sl, c=c, i=i: e.transpose(
                            out=psG[:, c, 4 * i:4 * i + 4], in_=X[:, sl], identity=idf[0:4, 0:4]),
                            reads=[(xn_ if xn_ == "qe" else f"{xn_}{c}"), "idf"], writes=["psG"])
                k.op("dve", lambda e: e.tensor_copy(out=G[:], in_=psG[:]), reads=["psG"], writes=["G"])
                k.op("pe", lambda e: e.matmul(psGl[:], lhsT=sel[:], rhs=G[:].rearrange("p c i -> p (c i)"),
                                              start=True, stop=True),
                     reads=["sel", "G"], writes=["psGl"])
                k.op("dve", lambda e: e.tensor_copy(out=Gl[:].rearrange("p c i -> p (c i)"), in_=psGl[:]),
                     reads=["psGl"], writes=["Gl"])
                k.dma("sp", gtm_d, G[:], reads=["G"])
                k.dma("sp", glast_d, Gl[:], reads=["Gl"])
                k.barrier()
                k.flush()
        if "mc" in phases:
            with ExitStack() as ph:
                def sb(name, shape, dt):
                    return ph.enter_context(nc.sbuf_tensor("sb5_" + name, list(shape), dt))

                def ps(name, shape, dt=F32):
                    return ph.enter_context(nc.psum_tensor("ps5_" + name, list(shape), dt))

                G = sb("G", [128, 32, 16], F32)
                Gl = sb("Gl", [128, 32, 16], F32)
                hng = sb("hng", [128, 2048], F32)
                tri = sb("tri", [128, 128], F32)
                idf = sb("idf", [128, 128], F32)
                zero = sb("zero", [128, 1], F32)
                epsc = sb("epsc", [128, 1], F32)
                bdf = sb("bdf", [128, 16, 128], F32)
                bdk = sb("bdk", [128, 16, 128], BF16)
                bdv = sb("bdv", [128, 16, 128], BF16)
                Cf = [sb(f"Cf{h}", [128, 4, 512], F32) for h in range(4)]
                Cb = [sb(f"Cb{h}", [128, 4, 512], BF16) for h in range(4)]
                nf = sb("nf", [128, 4, 4], F32)
                nbf = sb("nbf", [128, 4, 4], BF16)
                qT = [sb(f"qT{i}", [128, 16, 128], BF16) for i in range(2)]
                ksT = [sb(f"ksT{i}", [128, 16, 128], BF16) for i in range(2)]
                xcb = [sb(f"xcb{i}", [128, 16, 128], BF16) for i in range(2)]
                xmb = [sb(f"xmb{i}", [128, 16, 128], BF16) for i in range(2)]
                om = [sb(f"om{i}", [128, 2048], F32) for i in range(2)]
                zmT = [sb(f"zmT{i}", [128, 16, 128], F32) for i in range(2)]
                sxT = [sb(f"sxT{i}", [128, 16, 128], F32) for i in range(2)]
                ucol = [sb(f"ucol{i}", [128, 8], BF16) for i in range(2)]
                ktm = [sb(f"ktm{i}", [128, 512], BF16) for i in range(3)]
                v1 = [sb(f"v1{i}", [128, 512], BF16) for i in range(3)]
                v2 = [sb(f"v2{i}", [128, 512], BF16) for i in range(3)]
                Pm = [sb(f"Pm{i}", [128, 128], BF16) for i in range(3)]
                hs = [sb(f"hs{i}", [128, 512], F32) for i in range(3)]
                hn = [sb(f"hn{i}", [128, 512], F32) for i in range(3)]
                hg = [sb(f"hg{i}", [128, 512], F32) for i in range(3)]
                tt = [sb(f"tt{i}", [128, 4, 128], F32) for i in range(3)]
                hzs = [sb(f"hzs{i}", [128, 16, 128], BF16) for i in range(2)]
                sm = [sb(f"sm{i}", [128, 16], F32) for i in range(3)]
                pk = ps("k", [128, 512])
                pv = ps("v", [128, 512])
                pN = [ps(f"N{i}", [128, 512]) for i in range(2)]
                pkv = [ps(f"kv{i}", [128, 512]) for i in range(2)]
                pT = ps("T", [128, 4, 128])
                pm = ps("m", [128, 512])

                k.dma("sp", G[:], gtm_d, writes=["G"])
                k.dma("sp", Gl[:], glast_d, writes=["Gl"])
                k.dma("sp", hng[:], hng_d, writes=["hng"])
                k.dma("sp", tri[:], tri_d, writes=["tri"])
                k.dma("sp", idf[:], ident_d, writes=["idf"])
                k.dma("sp", bdf[:], bdk_d, writes=["bdf"])
                k.op("dve", lambda e: e.tensor_copy(out=bdk[:], in_=bdf[:]), reads=["bdf"], writes=["bdk"])
                k.dma("sp", bdf[:], bdv_d, writes=["bdf"])
                k.op("dve", lambda e: e.tensor_copy(out=bdv[:], in_=bdf[:]), reads=["bdf"], writes=["bdv"])
                k.op("pool", lambda e: e.memset(zero[:], 0.0), writes=["zero"])
                k.op("pool", lambda e: e.memset(epsc[:], EPS), writes=["epsc"])
                for h in range(4):
                    k.op("pool", lambda e, h=h: e.memset(Cf[h][:], 0.0), writes=[f"Cf{h}_{i}" for i in range(4)])
                    k.op("pool", lambda e, h=h: e.memset(Cb[h][:], 0.0), writes=[f"Cb{h}_{i}" for i in range(4)])
                k.op("pool", lambda e: e.memset(nf[:], 0.0), writes=[f"nf{h}" for h in range(4)])
                k.op("pool", lambda e: e.memset(nbf[:], 0.0), writes=[f"nbf{h}" for h in range(4)])
                KS = float(512 ** -0.5)

                def fm_tile(dd, c):
                    return dd[:, c * 128:(c + 1) * 128].rearrange("(fc p) t -> p fc t", p=128)

                def loads(c):
                    b = c % 2
                    k.dma("sp", qT[b][:], fm_tile(qmT_d, c), writes=[f"qT{b}"])
                    k.dma("sp", ksT[b][:], fm_tile(ksT_d, c), writes=[f"ksT{b}"])
                    k.dma("sp", xcb[b][:], fm_tile(xcbT_d, c), writes=[f"xcb{b}"])
                    k.dma("sp", xmb[b][:], fm_tile(xmbT_d, c), writes=[f"xmb{b}"])
                    k.dma("sp", om[b][:], om_d[c * 128:(c + 1) * 128, :], writes=[f"om{b}"])
                    k.dma("sp", zmT[b][:], fm_tile(zmT_d, c), writes=[f"zmT{b}"])
                    k.dma("sp", sxT[b][:], fm_tile(sxT_d, c), writes=[f"sxT{b}"])

                loads(0)
                NCH = int(os.environ.get('MC_NCH', '32'))
                units = [(c, h) for c in range(NCH) for h in range(4)]

                def stageA1(u):
                    c, h = units[u]
                    b = c % 2
                    ub = u % 3
                    if h == 0:
                        k.op("dve", lambda e: e.tensor_copy(out=ucol[b][:, 0:4], in_=G[:, c, 0:4]),
                             reads=["G"], writes=[f"ucol{b}"])
                        k.op("dve", lambda e: e.tensor_copy(out=ucol[b][:, 4:8], in_=G[:, c, 12:16]),
                             reads=["G"], writes=[f"ucol{b}"])
                    fcs = [4 * h + i for i in range(4)]
                    for i, fc in enumerate(fcs):
                        k.op("pe", lambda e, i=i, fc=fc: e.matmul(
                            pk[:, i * 128:(i + 1) * 128], lhsT=xcb[b][:, fc, :], rhs=bdk[:, fc, :],
                            start=True, stop=True), reads=[f"xcb{b}", "bdk"], writes=["pk"])
                    for i, fc in enumerate(fcs):
                        k.op("pe", lambda e, i=i, fc=fc: e.matmul(
                            pv[:, i * 128:(i + 1) * 128], lhsT=xmb[b][:, fc, :], rhs=bdv[:, fc, :],
                            start=True, stop=True), reads=[f"xmb{b}", "bdv"], writes=["pv"])
                    pS = pm[:, ub * 128:(ub + 1) * 128]
                    for i, fc in enumerate(fcs):
                        k.op("pe", lambda e, i=i, fc=fc: e.matmul(
                            pS, lhsT=ksT[b][:, fc, :], rhs=qT[b][:, fc, :], start=(i == 0), stop=(i == 3)),
                            reads=[f"ksT{b}", f"qT{b}"], writes=[f"pS{ub}"])
                    k.op("act", lambda e: e.activation(out=ktm[ub][:], in_=pk[:], func=AF.Identity, scale=KS),
                         reads=["pk"], writes=[f"ktm{ub}"])
                    k.op("act", lambda e: e.activation(
                        out=v1[ub][:], in_=pv[:], func=AF.Identity, scale=G[:, c, h:h + 1]),
                        reads=["pv", "G"], writes=[f"v1{ub}"])
                    k.op("act", lambda e: e.activation(
                        out=v2[ub][:], in_=pv[:], func=AF.Identity, scale=G[:, c, 12 + h:13 + h]),
                        reads=["pv", "G"], writes=[f"v2{ub}"])
                    k.op("dve", lambda e: e.tensor_tensor(out=Pm[ub][:], in0=pS, in1=tri[:], op=ALU.mult),
                         reads=[f"pS{ub}", "tri"], writes=[f"Pm{ub}"])

                def stageA2(u):
                    c, h = units[u]
                    b = c % 2
                    ub = u % 3
                    nb_ = u % 2
                    fcs = [4 * h + i for i in range(4)]
                    for i, fc in enumerate(fcs):
                        k.op("pe", lambda e, i=i, fc=fc: e.matmul(
                            pN[nb_][:], lhsT=qT[b][:, fc, :], rhs=Cb[h][:, i, :], start=(i == 0), stop=False),
                            reads=[f"qT{b}", f"Cb{h}_{i}"], writes=[f"pN{nb_}"])
                    k.op("pe", lambda e: e.matmul(pN[nb_][:], lhsT=Pm[ub][:], rhs=v1[ub][:], start=False, stop=True),
                         reads=[f"Pm{ub}", f"v1{ub}"], writes=[f"pN{nb_}"])
                    pd = pm[:, 400 + h:401 + h]
                    for i, fc in enumerate(fcs):
                        k.op("pe", lambda e, i=i, fc=fc: e.matmul(
                            pd, lhsT=qT[b][:, fc, :], rhs=nbf[:, h, i:i + 1], start=(i == 0), stop=False),
                            reads=[f"qT{b}", f"nbf{h}"], writes=[f"pd{h}"])
                    k.op("pe", lambda e: e.matmul(
                        pd, lhsT=Pm[ub][:], rhs=ucol[b][:, h:h + 1], start=False, stop=True),
                        reads=[f"Pm{ub}", f"ucol{b}"], writes=[f"pd{h}"])
                    dec = Gl[:, c, 4 + h:5 + h]
                    for i in range(4):
                        kvs = (u * 4 + i) % 2
                        k.op("pe", lambda e, i=i, kvs=kvs: e.matmul(
                            pkv[kvs][:], lhsT=ktm[ub][:, i * 128:(i + 1) * 128], rhs=v2[ub][:], start=True, stop=True),
                            reads=[f"ktm{ub}", f"v2{ub}"], writes=[f"pkv{kvs}"])
                        pkn = pm[:, 416 + 4 * h + i:417 + 4 * h + i]
                        k.op("pe", lambda e, i=i, pkn=pkn: e.matmul(
                            pkn, lhsT=ktm[ub][:, i * 128:(i + 1) * 128], rhs=ucol[b][:, 4 + h:5 + h],
                            start=True, stop=True),
                            reads=[f"ktm{ub}", f"ucol{b}"], writes=[f"pkn{h}"])
                        k.op("dve", lambda e, i=i, kvs=kvs: e.scalar_tensor_tensor(
                            out=Cf[h][:, i, :], in0=Cf[h][:, i, :], scalar=dec, in1=pkv[kvs][:],
                            op0=ALU.mult, op1=ALU.add),
                            reads=[f"Cf{h}_{i}", "Gl", f"pkv{kvs}"], writes=[f"Cf{h}_{i}"])
                        k.op("act", lambda e, i=i: e.copy(out=Cb[h][:, i, :], in_=Cf[h][:, i, :]),
                             reads=[f"Cf{h}_{i}"], writes=[f"Cb{h}_{i}"])
                    k.op("dve", lambda e: e.scalar_tensor_tensor(
                        out=nf[:, h, :], in0=nf[:, h, :], scalar=dec, in1=pm[:, 416 + 4 * h:420 + 4 * h],
                        op0=ALU.mult, op1=ALU.add),
                        reads=[f"nf{h}", "Gl", f"pkn{h}"], writes=[f"nf{h}"])
                    k.op("dve", lambda e: e.tensor_copy(out=nbf[:, h, :], in_=nf[:, h, :]),
                         reads=[f"nf{h}"], writes=[f"nbf{h}"])

                def stageB(u):
                    c, h = units[u]
                    b = c % 2
                    ub = u % 2
                    u3 = u % 3
                    s_ = sm[u3]
                    smn = f"sm{u3}"
                    pd = pm[:, 400 + h:401 + h]
                    wq = G[:, c, 4 + h:5 + h]
                    ebq = G[:, c, 8 + h:9 + h]
                    k.op("dve", lambda e: e.tensor_scalar(
                        out=s_[:, 14:15], in0=pd, scalar1=wq, scalar2=None, op0=ALU.mult),
                        reads=[f"pd{h}", "G"], writes=[smn])
                    k.op("dve", lambda e: e.scalar_tensor_tensor(
                        out=s_[:, 0:1], in0=s_[:, 14:15], scalar=-1.0, in1=s_[:, 14:15],
                        op0=ALU.mult, op1=ALU.max),
                        reads=[smn], writes=[smn])
                    k.op("dve", lambda e: e.tensor_tensor(out=s_[:, 1:2], in0=s_[:, 0:1], in1=ebq, op=ALU.max),
                         reads=[smn, "G"], writes=[smn])
                    k.op("dve", lambda e: e.reciprocal(out=s_[:, 2:3], in_=s_[:, 1:2]), reads=[smn], writes=[smn])
                    k.op("dve", lambda e: e.tensor_tensor(out=s_[:, 3:4], in0=s_[:, 2:3], in1=wq, op=ALU.mult),
                         reads=[smn, "G"], writes=[smn])
                    k.op("dve", lambda e: e.scalar_tensor_tensor(
                        out=hs[u3][:], in0=pN[ub][:], scalar=s_[:, 3:4], in1=om[b][:, h * 512:(h + 1) * 512],
                        op0=ALU.mult, op1=ALU.mult),
                        reads=[f"pN{ub}", smn, f"om{b}"], writes=[f"hs{u3}"])
                    k.op("dve", lambda e: e.bn_stats(out=s_[:, 4:10], in_=hs[u3][:]), reads=[f"hs{u3}"], writes=[smn])
                    k.op("dve", lambda e: e.bn_aggr(out=s_[:, 10:12], in_=s_[:, 4:10]), reads=[smn], writes=[smn])
                    k.op("act", lambda e: e.activation(out=s_[:, 12:13], in_=s_[:, 11:12], func=AF.Sqrt,
                                                       bias=epsc[:, 0:1]),
                         reads=[smn, "epsc"], writes=[smn])
                    k.op("dve", lambda e: e.reciprocal(out=s_[:, 13:14], in_=s_[:, 12:13]), reads=[smn], writes=[smn])
                    k.op("dve", lambda e: e.scalar_tensor_tensor(
                        out=s_[:, 15:16], in0=s_[:, 10:11], scalar=-1.0, in1=s_[:, 13:14],
                        op0=ALU.mult, op1=ALU.mult),
                        reads=[smn], writes=[smn])
                    k.op("act", lambda e: e.activation(
                        out=hn[u3][:], in_=hs[u3][:], func=AF.Identity, scale=s_[:, 13:14], bias=s_[:, 15:16]),
                        reads=[f"hs{u3}", smn], writes=[f"hn{u3}"])
                    k.op("dve", lambda e: e.tensor_tensor(
                        out=hg[u3][:], in0=hn[u3][:], in1=hng[:, h * 512:(h + 1) * 512], op=ALU.mult),
                        reads=[f"hn{u3}", "hng"], writes=[f"hg{u3}"])

                def stageC(u):
                    c, h = units[u]
                    b = c % 2
                    u3 = u % 3
                    for i in range(4):
                        k.op("pe", lambda e, i=i: e.transpose(
                            out=pT[:, i, :], in_=hg[u3][:, i * 128:(i + 1) * 128], identity=idf[:]),
                            reads=[f"hg{u3}", "idf"], writes=["pT"])
                    k.op("dve", lambda e: e.tensor_tensor(
                        out=tt[u3][:], in0=pT[:], in1=sxT[b][:, 4 * h:4 * h + 4, :], op=ALU.add),
                        reads=["pT", f"sxT{b}"], writes=[f"tt{u3}"])
                    k.op("dve", lambda e: e.tensor_tensor(
                        out=hzs[b][:, 4 * h:4 * h + 4, :], in0=tt[u3][:], in1=zmT[b][:, 4 * h:4 * h + 4, :],
                        op=ALU.mult),
                        reads=[f"tt{u3}", f"zmT{b}"], writes=[f"hzs{b}_{h}"])
                    if h == 3:
                        k.dma("sp", fm_tile(hzT_d, c), hzs[b][:], reads=[f"hzs{b}_{hh}" for hh in range(4)])

                NU = len(units)
                for idx in range(NU + 3):
                    lists = []
                    if idx < NU:
                        lists.append(k.collect(stageA1, idx))
                    if 0 <= idx - 1 < NU:
                        lists.append(k.collect(stageA2, idx - 1))
                    if 0 <= idx - 2 < NU:
                        lists.append(k.collect(stageB, idx - 2))
                    if 0 <= idx - 3 < NU:
                        lists.append(k.collect(stageC, idx - 3))
                    if os.environ.get("MC_INTERLEAVE", "1") == "1":
                        k.emit_interleaved(lists)
                    else:
                        for l in lists:
                            for it in l:
                                k.op(*it)
                    if idx % 4 == 2:
                        cn = idx // 4 + 1
                        if cn < 32:
                            loads(cn)
                k.barrier()
                k.flush()
        if "p3" in phases:
            with ExitStack() as ph:
                def sb(name, shape, dt):
                    return ph.enter_context(nc.sbuf_tensor("sb6_" + name, list(shape), dt))

                def ps(name, shape, dt=F32):
                    return ph.enter_context(nc.psum_tensor("ps6_" + name, list(shape), dt))

                wpa = sb("wpa", [128, 8, 1024], BF16)
                wpb = sb("wpb", [128, 16, 1024], BF16)
                wout = sb("wout", [128, 8, 1024], BF16)
                gout = sb("gout", [128, 1024], F32)
                ident = sb("ident", [128, 128], BF16)
                identf = sb("identf", [128, 128], F32)
                epsc = sb("epsc", [128, 1], F32)
                hzT = [sb(f"hzT{i}", [128, 16, 512], BF16) for i in range(2)]
                gr = [sb(f"gr{i}", [128, 512], F32) for i in range(4)]
                at = [sb(f"at{i}", [128, 16, 65], F32) for i in range(6)]
                za = [sb(f"za{i}", [128, 1024], F32) for i in range(2)]
                xt = [sb(f"xt{i}", [128, 1024], F32) for i in range(2)]
                rden = [sb(f"rden{i}", [128, 16], F32) for i in range(2)]
                yazb = [sb(f"yazb{i}", [128, 1024], BF16) for i in range(2)]
                yazT = [sb(f"yazT{i}", [128, 8, 512], BF16) for i in range(2)]
                m1 = [sb(f"m1{i}", [128, 512], F32) for i in range(2)]
                m2 = [sb(f"m2{i}", [128, 512], F32) for i in range(2)]
                mT = [sb(f"mT{i}", [128, 8, 512], BF16) for i in range(2)]
                hh = [sb(f"hh{i}", [128, 1024], F32) for i in range(2)]
                sq = sb("sq", [128, 1024], F32)
                st = sb("st", [128, 64], F32)
                psT = ps("T", [128, 1024], BF16)
                pya = [ps(f"ya{i}", [128, 512]) for i in range(2)]
                pym = [ps(f"ym{i}", [128, 512]) for i in range(2)]
                po = [ps(f"o{i}", [128, 512]) for i in range(2)]

                k.dma("pool", wpa[:], wpa_d.rearrange("(kc p) c -> p kc c", p=128), writes=["wpa"])
                k.dma("pool", wpb[:], wpb_d.rearrange("(kc p) c -> p kc c", p=128), writes=["wpb"])
                k.dma("pool", wout[:], wout_d.rearrange("(kc p) c -> p kc c", p=128), writes=["wout"])
                k.dma("sp", gout[:], gout_d, writes=["gout"])
                k.dma("sp", identf[:], ident_d, writes=["identf"])
                k.op("dve", lambda e: e.tensor_copy(out=ident[:], in_=identf[:]), reads=["identf"], writes=["ident"])
                k.op("pool", lambda e: e.memset(epsc[:], EPS), writes=["epsc"])

                def fm_grp(dd, g):
                    return dd[:, g * 512:(g + 1) * 512].rearrange("(fc p) t -> p fc t", p=128)

                state = {"gri": 0, "tix": 0, "t1": 0}

                def s1_tile(g, tt4):
                    t = g * 4 + tt4
                    b = t % 2
                    yz = yazT[g % 2]
                    yzn = f"yazT{g % 2}"
                    rows = slice(t * 128, (t + 1) * 128)
                    ats = [at[3 * b + p_] for p_ in range(3)]
                    atn = [f"at{3 * b + p_}" for p_ in range(3)]
                    for p_ in range(3):
                        k.dma("sp", ats[p_][:], att_d[p_][rows, :, :], writes=[atn[p_]])
                    k.dma("sp", za[b][:], za_d[rows, :], writes=[f"za{b}"])
                    k.op("dve", lambda e: e.tensor_tensor(out=ats[0][:], in0=ats[0][:], in1=ats[1][:], op=ALU.add),
                         reads=[atn[0], atn[1]], writes=[atn[0]])
                    k.op("dve", lambda e: e.tensor_tensor(out=ats[0][:], in0=ats[0][:], in1=ats[2][:], op=ALU.add),
                         reads=[atn[0], atn[2]], writes=[atn[0]])
                    k.op("dve", lambda e: e.reciprocal(out=rden[b][:], in_=ats[0][:, :, 64]),
                         reads=[atn[0]], writes=[f"rden{b}"])
                    for h in range(16):
                        k.op("dve", lambda e, h=h: e.scalar_tensor_tensor(
                            out=yazb[b][:, h * 64:(h + 1) * 64], in0=ats[0][:, h, 0:64], scalar=rden[b][:, h:h + 1],
                            in1=za[b][:, h * 64:(h + 1) * 64], op0=ALU.mult, op1=ALU.mult),
                            reads=[atn[0], f"rden{b}", f"za{b}"], writes=[f"yazb{b}_{h}"])
                    for c in range(8):
                        k.op("pe", lambda e, c=c: e.transpose(
                            out=psT[:, c * 128:(c + 1) * 128], in_=yazb[b][:, c * 128:(c + 1) * 128],
                            identity=ident[:]),
                            reads=[f"yazb{b}_{2 * c}", f"yazb{b}_{2 * c + 1}", "ident"], writes=["psT"])
                    k.op("act", lambda e: e.copy(
                        out=yz[:, :, tt4 * 128:(tt4 + 1) * 128],
                        in_=psT[:].rearrange("p (c t) -> p c t", c=8)),
                        reads=["psT"], writes=[yzn])

                def s2_chunk(g, j):
                    hb = g % 2
                    pb = j % 2
                    yz = yazT[g % 2]
                    yzn = f"yazT{g % 2}"
                    gri = state["gri"]
                    ga = gr[gri % 4]
                    gan = f"gr{gri % 4}"
                    gm = gr[(gri + 1) % 4]
                    gmn = f"gr{(gri + 1) % 4}"
                    state["gri"] = gri + 2
                    k.dma("sp", ga[:], gT_d[j * 128:(j + 1) * 128, g * 512:(g + 1) * 512], writes=[gan])
                    k.dma("sp", gm[:], gT_d[1024 + j * 128:1024 + (j + 1) * 128, g * 512:(g + 1) * 512],
                          writes=[gmn])
                    for kc in range(8):
                        k.op("pe", lambda e, kc=kc: e.matmul(
                            pya[pb][:], lhsT=wpa[:, kc, j * 128:(j + 1) * 128], rhs=yz[:, kc, :],
                            start=(kc == 0), stop=(kc == 7)),
                            reads=["wpa", yzn], writes=[f"pya{pb}"])
                    for kc in range(16):
                        k.op("pe", lambda e, kc=kc: e.matmul(
                            pym[pb][:], lhsT=wpb[:, kc, j * 128:(j + 1) * 128], rhs=hzT[hb][:, kc, :],
                            start=(kc == 0), stop=(kc == 15)),
                            reads=["wpb", f"hzT{hb}"], writes=[f"pym{pb}"])
                    k.op("dve", lambda e: e.tensor_tensor(out=m1[pb][:], in0=pya[pb][:], in1=ga[:], op=ALU.mult),
                         reads=[f"pya{pb}", gan], writes=[f"m1{pb}"])
                    k.op("dve", lambda e: e.tensor_tensor(out=m2[pb][:], in0=pym[pb][:], in1=gm[:], op=ALU.mult),
                         reads=[f"pym{pb}", gmn], writes=[f"m2{pb}"])
                    k.op("dve", lambda e: e.tensor_tensor(out=mT[g % 2][:, j, :], in0=m1[pb][:], in1=m2[pb][:], op=ALU.add),
                         reads=[f"m1{pb}", f"m2{pb}"], writes=[f"mT{g % 2}_{j}"])

                def s3_tile(g, tt4):
                    t = g * 4 + tt4
                    b = t % 2
                    rows = slice(t * 128, (t + 1) * 128)
                    mT_all = [f"mT{g % 2}_{j}" for j in range(8)]
                    k.dma("sp", xt[b][:], x_d[rows, :], writes=[f"xt{b}"])
                    for half in range(2):
                        for kc in range(8):
                            k.op("pe", lambda e, half=half, kc=kc: e.matmul(
                                po[half][:], lhsT=mT[g % 2][:, kc, tt4 * 128:(tt4 + 1) * 128],
                                rhs=wout[:, kc, half * 512:(half + 1) * 512], start=(kc == 0), stop=(kc == 7)),
                                reads=["wout"] + mT_all, writes=[f"po{half}"])
                        k.op("dve", lambda e, half=half: e.tensor_tensor(
                            out=hh[b][:, half * 512:(half + 1) * 512], in0=po[half][:],
                            in1=xt[b][:, half * 512:(half + 1) * 512], op=ALU.add),
                            reads=[f"po{half}", f"xt{b}"], writes=[f"hh{b}"])
                    c_ = state["tix"] % 16
                    state["tix"] += 1
                    k.op("act", lambda e: e.activation(
                        out=sq[:], in_=hh[b][:], func=AF.Square, accum_out=st[:, c_:c_ + 1]),
                        reads=[f"hh{b}"], writes=["sq", f"st{c_}"])
                    k.op("act", lambda e: e.activation(
                        out=st[:, 16 + c_:17 + c_], in_=st[:, c_:c_ + 1], func=AF.Sqrt, scale=1.0 / D,
                        bias=epsc[:, 0:1]),
                        reads=[f"st{c_}", "epsc"], writes=[f"st{16 + c_}"])
                    k.op("dve", lambda e: e.reciprocal(out=st[:, 32 + c_:33 + c_], in_=st[:, 16 + c_:17 + c_]),
                         reads=[f"st{16 + c_}"], writes=[f"st{32 + c_}"])
                    k.op("dve", lambda e: e.scalar_tensor_tensor(
                        out=hh[b][:], in0=hh[b][:], scalar=st[:, 32 + c_:33 + c_], in1=gout[:],
                        op0=ALU.mult, op1=ALU.mult),
                        reads=[f"hh{b}", f"st{32 + c_}", "gout"], writes=[f"hh{b}"])
                    k.dma("sp", out_d[rows, :], hh[b][:], reads=[f"hh{b}"])

                k.dma("sp", hzT[0][:], fm_grp(hzT_d, 0), writes=["hzT0"])
                for tt4 in range(4):
                    s1_tile(0, tt4)
                for g in range(9):
                    hb = g % 2
                    if g + 1 < 8:
                        k.dma("sp", hzT[1 - hb][:], fm_grp(hzT_d, g + 1), writes=[f"hzT{1 - hb}"])
                    for jj in range(4):
                        lists = []
                        if g < 8:
                            lists.append(k.collect(s2_chunk, g, 2 * jj))
                            lists.append(k.collect(s2_chunk, g, 2 * jj + 1))
                        if g + 1 < 8:
                            lists.append(k.collect(s1_tile, g + 1, jj))
                        if g >= 1:
                            lists.append(k.collect(s3_tile, g - 1, jj))
                        k.emit_interleaved(lists)
                k.barrier()
                k.flush()
        k.barrier()
        k.op("sp", lambda e: e.sem_inc(k.sem["sp"], 1))
        k.prog["sp"][-1]["selfinc"] = True
        k.flush()
        if os.environ.get("SIM", "0") == "1":
            print("simulate ok:", k.simulate())
    return nc


def t5_bucket_np(dist):
    f = np.float32
    dist = dist.astype(np.int32)
    large = 16 + (np.log(np.maximum(dist, 16).astype(f) / f(16)) / f(math.log(2048 / 16)) * f(16)).astype(np.int32)
    return np.where(dist < 16, dist, np.minimum(large, 31))


_ATT_CACHE = {}


def attn_tables(rel_bias):
    key = id(rel_bias)
    if key in _ATT_CACHE:
        return _ATT_CACHE[key]
    rel_bias = np.asarray(rel_bias, dtype=np.float32)
    kk = np.arange(128)[:, None, None]
    cc = np.arange(2)[None, :, None]
    qq = np.arange(128)[None, None, :]
    delta = qq + 128 - (kk + 128 * cc)
    mask = ((delta >= 0) & (delta <= 128)).astype(np.float32)
    bt = np.zeros((3, 128, 16, 2, 128), np.float32)
    for p, d in enumerate((1, 4, 16)):
        bucket = t5_bucket_np(np.clip(delta, 0, 128) * d)
        g = rel_bias[bucket]
        bt[p] = np.transpose(g, (0, 3, 1, 2))
    _ATT_CACHE[key] = (np.ascontiguousarray(bt), np.ascontiguousarray(mask))
    return _ATT_CACHE[key]


def block_diag_layout(w):
    w = np.asarray(w, dtype=np.float32).reshape(16, 32, 4, 4)
    out = np.zeros((16, 32, 4, 32, 4), np.float32)
    for gi in range(32):
        out[:, gi, :, gi, :] = w[:, gi]
    out = out.reshape(16, 128, 128).transpose(1, 0, 2)
    return np.ascontiguousarray(out)


def host_inputs(inputs, b):
    f = np.float32
    d = {}
    d["x"] = np.ascontiguousarray(inputs["x"][b], dtype=f)
    d["w_in"] = np.ascontiguousarray(inputs["w_in"][0], dtype=f)
    d["gin_bc"] = np.ascontiguousarray(np.broadcast_to(inputs["norm_in_g"][0][None, :], (128, D)), dtype=f)
    bt, mt = attn_tables(inputs["rel_bias"])
    d["bias_tab"] = bt
    d["mask_tab"] = mt
    d["cw"] = np.ascontiguousarray(inputs["conv_w"][0].reshape(4, 16, 128).transpose(2, 1, 0), dtype=f)
    d["cb"] = np.ascontiguousarray(inputs["conv_b"][0].reshape(16, 128).T, dtype=f)
    d["bif"] = np.ascontiguousarray(inputs["b_if"][0].reshape(8, 1), dtype=f)
    d["wif"] = np.ascontiguousarray(inputs["w_if"][0].reshape(48, 128, 8).transpose(1, 0, 2), dtype=f)
    for nm, key in (("bdq", "wq_m"), ("bdk", "wk_m"), ("bdv", "wv_m")):
        d[nm] = block_diag_layout(inputs[key][0])
    d["w_pa"] = np.ascontiguousarray(inputs["w_pa"][0], dtype=f)
    d["w_pb"] = np.ascontiguousarray(inputs["w_pb"][0], dtype=f)
    d["w_out"] = np.ascontiguousarray(inputs["w_out"][0], dtype=f)
    d["gout_bc"] = np.ascontiguousarray(np.broadcast_to(inputs["norm_out_g"][None, :], (128, D)), dtype=f)
    d["skip_fm"] = np.ascontiguousarray(inputs["skip_m"][0].reshape(16, 128).T, dtype=f)
    d["hng_bc"] = np.ascontiguousarray(np.broadcast_to(inputs["head_norm_g"][0][None, :], (128, 2048)), dtype=f)
    d["tri"] = np.ascontiguousarray(np.triu(np.ones((128, 128), f)))
    sel = np.zeros((128, 128), f)
    sel[127, :] = 1.0
    d["sel127"] = sel
    d["ident"] = np.eye(128, dtype=f)
    d["gate_b_fm"] = np.ascontiguousarray(inputs["gate_b"][0].reshape(16, 128).T, dtype=f)
    return d


def kernel(**inputs):
    inputs = {k_: np.asarray(v) for k_, v in inputs.items()}
    nc = build()
    in_maps = [host_inputs(inputs, b) for b in range(NCORES)]
    res = run_bass_kernel_spmd(nc, in_maps, core_ids=list(range(NCORES)))
    out = np.stack([np.asarray(r["out"]) for r in res.results], axis=0)
    return out.astype(np.float32)
```

```python
import math
import os
from contextlib import ExitStack

import numpy as np
import concourse.bass as bass
import concourse.mybir as mybir
from concourse.bass_utils import run_bass_kernel_spmd

F32 = mybir.dt.float32
BF16 = mybir.dt.bfloat16
ALU = mybir.AluOpType
AF = mybir.ActivationFunctionType
AX = mybir.AxisListType

S = 4096
D = 1024
NT = S // 128
N_IN = 12288
EPS = 1e-6
NCORES = 8


class Sched:
    ENG = ["pe", "act", "dve", "pool", "sp"]
    NDMA = {"sp": 24, "pool": 12, "act": 8}

    def __init__(self, nc, stack):
        self.nc = nc
        self.e = dict(pe=nc.tensor, act=nc.scalar, dve=nc.vector, pool=nc.gpsimd, sp=nc.sync)
        self.prog = {k: [] for k in self.ENG}
        self.lastw = {}
        self.readers = {}
        self.dma_k = {q: 0 for q in self.NDMA}
        self.dma_recent = {q: [] for q in self.NDMA}
        self.extra = {k: set() for k in self.ENG}
        self.force = set()
        self._collect = None
        self.sem = {k: stack.enter_context(nc.semaphore("s_" + k)) for k in self.ENG}
        self.dsem = {q: [stack.enter_context(nc.semaphore(f"d_{q}{i}")) for i in range(n)]
                     for q, n in self.NDMA.items()}
        self.emitted = {k: 0 for k in self.ENG}
        self.cum = {k: 0 for k in self.ENG}
        self.cumat = {k: {} for k in self.ENG}
        self.seen = {k: {x: -1 for x in self.ENG} for k in self.ENG}
        self.seen_dma = {k: set() for k in self.ENG}
        self.know = {k: [] for k in self.ENG}

    def collect(self, stage_fn, *args):
        self._collect = []
        stage_fn(*args)
        lst = self._collect
        self._collect = None
        return lst

    def emit_interleaved(self, lists):
        for it in self.merge_lists(lists):
            self.op(*it)

    def merge_lists(self, lists):
        items = []
        for li, l in enumerate(lists):
            n = len(l)
            prev = None
            ppos = 0.0
            for j, it in enumerate(l):
                pos = (j + 0.5) / n
                if it[0] == "pe" and prev is not None and prev[0] == "pe" and prev[3] == it[3]:
                    pos = ppos
                items.append((pos, li, j, it))
                prev = it
                ppos = pos
        items.sort(key=lambda t: (t[0], t[1]))
        return [it for _, _, _, it in items]

    def op(self, eng, fn, reads=(), writes=(), dma=False):
        if getattr(self, "_collect", None) is not None:
            self._collect.append((eng, fn, tuple(reads), tuple(writes), dma))
            return None
        deps = set(self.extra[eng])
        self.extra[eng] = set()
        for r in reads:
            w = self.lastw.get(r)
            if w is not None:
                deps.add(w)
        for w_ in writes:
            w = self.lastw.get(w_)
            if w is not None:
                deps.add(w)
            for rd in self.readers.get(w_, ()):
                deps.add(rd)
        idx = len(self.prog[eng])
        if dma:
            k = self.dma_k[eng]
            self.dma_k[eng] += 1
            tok = ("dma", eng, k)
            n = self.NDMA[eng]
            if k >= n:
                deps.add(("dma", eng, k - n))
            self.dma_recent[eng].append(tok)
        else:
            tok = ("eng", eng, idx)
        self.prog[eng].append(dict(fn=fn, deps=deps, tok=tok, dma=dma))
        for r in reads:
            self.readers.setdefault(r, []).append(tok)
        for w_ in writes:
            self.lastw[w_] = tok
            self.readers[w_] = []
        return tok

    def simulate(self):
        vals = {}
        ptr = {k: 0 for k in self.ENG}
        progress = True
        while progress:
            progress = False
            for k in self.ENG:
                st = self.stream[k]
                while ptr[k] < len(st):
                    wl, inc = st[ptr[k]]
                    if all(vals.get(sk, 0) >= v for sk, v in wl):
                        if inc is not None:
                            vals[inc[0]] = vals.get(inc[0], 0) + inc[1]
                        ptr[k] += 1
                        progress = True
                    else:
                        break
        stuck = {k: (ptr[k], len(self.stream[k])) for k in self.ENG if ptr[k] < len(self.stream[k])}
        for k, (p, n) in stuck.items():
            wl, inc = self.stream[k][p]
            print("STUCK", k, p, n, [(sk, v, vals.get(sk, 0)) for sk, v in wl])
        return not stuck

    def dma(self, q, out, in_, reads=(), writes=()):
        return self.op(q, lambda e: e.dma_start(out=out, in_=in_), reads, writes, dma=True)

    def barrier(self):
        toks = set()
        for q in self.NDMA:
            if not self.dma_recent[q]:
                continue
            recent = self.dma_recent[q][-self.NDMA[q]:]
            self.dma_recent[q] = []
            sem = self.sem[q]
            self.extra[q] |= set(recent)
            for i in range(len(self.prog[q]) - 1, -1, -1):
                ins = self.prog[q][i]
                if not ins["dma"] and not ins.get("selfinc"):
                    self.extra[q].add(ins["tok"])
                    break
            t = self.op(q, lambda e, sem=sem: e.sem_inc(sem, 1), (), ())
            self.prog[q][-1]["selfinc"] = True
            toks.add(t)
        for k in self.ENG:
            if k in ("sp",):
                continue
            for i in range(len(self.prog[k]) - 1, -1, -1):
                ins = self.prog[k][i]
                if not ins["dma"] and not ins.get("selfinc"):
                    toks.add(ins["tok"])
                    break
        for k in self.ENG:
            self.extra[k] |= toks
        self.force |= {t for t in toks if t[0] == "eng"}

    def flush(self):
        start = dict(self.emitted)
        waits = {k: {} for k in self.ENG}
        marked = {k: set() for k in self.ENG}
        ptr = dict(start)
        n_total = {k: len(self.prog[k]) for k in self.ENG}
        for t in self.force:
            if t[2] >= start[t[1]]:
                marked[t[1]].add(t[2])
        self.force = set()
        progress = True
        while progress:
            progress = False
            for k in self.ENG:
                while ptr[k] < n_total[k]:
                    i = ptr[k]
                    ins = self.prog[k][i]
                    ok = True
                    for d in ins["deps"]:
                        if d[0] == "eng" and d[1] != k and d[2] >= ptr[d[1]]:
                            ok = False
                            break
                    if not ok:
                        break
                    need_eng = {}
                    need_dma = []
                    for d in ins["deps"]:
                        if d[0] == "eng":
                            x, j = d[1], d[2]
                            if x == k and k in ("pe", "sp"):
                                continue
                            if x == k and j >= i:
                                continue
                            if self.seen[k][x] >= j:
                                continue
                            if need_eng.get(x, -1) < j:
                                need_eng[x] = j
                        else:
                            if d in self.seen_dma[k]:
                                continue
                            need_dma.append(d)
                    for x, j in need_eng.items():
                        marked[x].add(j)
                        kn = self.know[x][j]
                        for y, v in kn.items():
                            if self.seen[k][y] < v:
                                self.seen[k][y] = v
                        if self.seen[k][x] < j:
                            self.seen[k][x] = j
                    for d in need_dma:
                        self.seen_dma[k].add(d)
                    waits[k][i] = (need_eng, need_dma)
                    kn = dict(self.seen[k])
                    if not ins["dma"]:
                        kn[k] = max(kn[k], i - 1)
                    self.know[k].append(kn)
                    ptr[k] += 1
                    progress = True
        for k in self.ENG:
            assert ptr[k] == n_total[k], f"deadlock in dependency graph on {k} at {ptr[k]}"
        for k in self.ENG:
            c = self.cum[k]
            for i in range(start[k], n_total[k]):
                ins = self.prog[k][i]
                if ins.get("selfinc") or (i in marked[k] and not ins["dma"]):
                    c += 1
                    self.cumat[k][i] = c
            self.cum[k] = c
        sched = self

        if not hasattr(self, "stream"):
            self.stream = {k: [] for k in self.ENG}

        def run(k, eng):
            for i in range(start[k], n_total[k]):
                ins = sched.prog[k][i]
                need_eng, need_dma = waits[k][i]
                wl = []
                for x, j in need_eng.items():
                    eng.wait_ge(sched.sem[x], sched.cumat[x][j])
                    wl.append((("e", x), sched.cumat[x][j]))
                for d in need_dma:
                    _, q, kk = d
                    n = sched.NDMA[q]
                    eng.wait_ge(sched.dsem[q][kk % n], 16 * (kk // n + 1))
                    wl.append((("d", q, kk % n), 16 * (kk // n + 1)))
                r = ins["fn"](eng)
                inc = None
                if ins["dma"]:
                    _, q, kk = ins["tok"]
                    r.then_inc(sched.dsem[q][kk % sched.NDMA[q]], 16)
                    inc = (("d", q, kk % sched.NDMA[q]), 16)
                elif ins.get("selfinc"):
                    inc = (("e", k), 1)
                elif i in marked[k]:
                    r.then_inc(sched.sem[k], 1)
                    inc = (("e", k), 1)
                sched.stream[k].append((wl, inc))

        with self.nc.Block() as block:
            @block.tensor
            def _(eng):
                run("pe", eng)

            @block.scalar
            def _(eng):
                run("act", eng)

            @block.vector
            def _(eng):
                run("dve", eng)

            @block.gpsimd
            def _(eng):
                run("pool", eng)

            @block.sync
            def _(eng):
                run("sp", eng)
        for k in self.ENG:
            self.emitted[k] = n_total[k]


ALL_PHASES = ("p1", "p2", "ma", "mb", "mc", "p3")
MERGE_MA = os.environ.get("MERGE_MA", "1") == "1"


def build(debug_outs=(), phases=ALL_PHASES):
    nc = bass.Bass("TRN2", target_bir_lowering=False)
    dbg = set(debug_outs)

    def dram_in(name, shape, dt=F32):
        return nc.dram_tensor(name, list(shape), dt, kind="ExternalInput").ap()

    def scratch(name, shape, dt):
        kind = "ExternalOutput" if name in dbg else "Internal"
        return nc.dram_tensor(name, list(shape), dt, kind=kind).ap()

    x_d = dram_in("x", [S, D])
    w_in_d = dram_in("w_in", [D, N_IN])
    gin_d = dram_in("gin_bc", [128, D])
    gateb_d = dram_in("gate_b_fm", [128, 16])
    ident_d = dram_in("ident", [128, 128])
    out_d = nc.dram_tensor("out", [S, D], F32, kind="ExternalOutput").ap()

    bias_d = dram_in("bias_tab", [3, 128, 16, 2, 128])
    mask_d = dram_in("mask_tab", [128, 2, 128])
    att_d = [scratch(f"att{p}", [S, 16, 65], F32) for p in range(3)]
    cw_d = dram_in("cw", [128, 16, 4])
    cb_d = dram_in("cb", [128, 16])
    bif_d = dram_in("bif", [8, 1])
    wif_d = dram_in("wif", [128, 48, 8])
    bdq_d = dram_in("bdq", [128, 16, 128])
    bdk_d = dram_in("bdk", [128, 16, 128])
    bdv_d = dram_in("bdv", [128, 16, 128])
    sel_d = dram_in("sel127", [128, 128])
    wpa_d = dram_in("w_pa", [1024, 1024])
    wpb_d = dram_in("w_pb", [2048, 1024])
    wout_d = dram_in("w_out", [1024, 1024])
    gout_d = dram_in("gout_bc", [128, 1024])
    skip_d = dram_in("skip_fm", [128, 16])
    hng_d = dram_in("hng_bc", [128, 2048])
    tri_d = dram_in("tri", [128, 128])
    sxT_d = scratch("sxT", [2048, S], F32)
    hzT_d = scratch("hzT", [2048, S], BF16)
    xcbT_d = scratch("xcbT", [2048, S], BF16)
    xmbT_d = scratch("xmbT", [2048, S], BF16)
    qmT_d = scratch("qmT", [2048, S], BF16)
    ksT_d = scratch("ksT", [2048, S], BF16)
    gpre_d = scratch("gpre", [8, S], F32)
    gtm_d = scratch("gtm", [128, 32, 16], F32)
    glast_d = scratch("glast", [128, 32, 16], F32)
    qaT_d = scratch("qaT", [D, S], BF16)
    kaT_d = scratch("kaT", [D, S], BF16)
    va_d = scratch("va1", [S, 16, 65], BF16)
    za_d = scratch("za", [S, D], F32)
    xmT_d = scratch("xmT", [2048, S], F32)
    zmT_d = scratch("zmT", [2048, S], F32)
    om_d = scratch("om", [S, 2048], F32)
    gT_d = scratch("gT", [2048, S], F32)

    with ExitStack() as top:
        k = Sched(nc, top)
        if "p1" in phases:
            with ExitStack() as ph:
                def sb(name, shape, dt):
                    return ph.enter_context(nc.sbuf_tensor("sb_" + name, list(shape), dt))

                def ps(name, shape, dt=F32):
                    return ph.enter_context(nc.psum_tensor("ps_" + name, list(shape), dt))

                xnT = sb("xnT", [128, 8, S], BF16)
                xt = [sb(f"xt{i}", [128, D], F32) for i in range(2)]
                sq = sb("sq", [128, D], F32)
                xnb = [sb(f"xnb{i}", [128, D], BF16) for i in range(2)]
                gin = sb("gin", [128, D], F32)
                gateb = sb("gateb", [128, 16], F32)
                ident = sb("ident", [128, 128], BF16)
                identf = sb("identf", [128, 128], F32)
                ss = sb("ss", [128, NT], F32)
                rstd = sb("rstd", [128, NT], F32)
                wbf = [sb(f"wbf{i}", [128, 8, 512], BF16) for i in range(2)]
                stg = [sb(f"stg{i}", [128, 4096], F32) for i in range(2)]
                psT = [ps(f"psT{i}", [128, 1024], BF16) for i in range(2)]
                psM = [ps(f"psM{i}", [128, 512], F32) for i in range(4)]

                k.dma("sp", gin[:], gin_d, writes=["gin"])
                k.dma("sp", gateb[:], gateb_d, writes=["gateb"])
                k.dma("sp", identf[:], ident_d, writes=["identf"])
                k.op("dve", lambda e: e.tensor_copy(out=ident[:], in_=identf[:]),
                     reads=["identf"], writes=["ident"])

                for t in range(NT):
                    b = t % 2
                    k.dma("sp", xt[b][:], x_d[t * 128:(t + 1) * 128, :], writes=[f"xt{b}"])
                    k.op("act", lambda e, b=b, t=t: e.activation(
                        out=sq[:], in_=xt[b][:], func=AF.Square, accum_out=ss[:, t:t + 1]),
                        reads=[f"xt{b}"], writes=["sq", f"ss{t}"])
                    k.op("act", lambda e, t=t: e.activation(
                        out=rstd[:, t:t + 1], in_=ss[:, t:t + 1], func=AF.Sqrt, scale=1.0 / D, bias=EPS),
                        reads=[f"ss{t}"], writes=[f"rstd{t}"])
                    k.op("dve", lambda e, t=t: e.reciprocal(out=rstd[:, t:t + 1], in_=rstd[:, t:t + 1]),
                         reads=[f"rstd{t}"], writes=[f"rstd{t}"])
                    k.op("dve", lambda e, b=b, t=t: e.scalar_tensor_tensor(
                        out=xnb[b][:], in0=xt[b][:], scalar=rstd[:, t:t + 1], in1=gin[:],
                        op0=ALU.mult, op1=ALU.mult),
                        reads=[f"xt{b}", f"rstd{t}", "gin"], writes=[f"xnb{b}"])
                    for c in range(8):
                        k.op("pe", lambda e, b=b, c=c: e.transpose(
                            out=psT[b][:, c * 128:(c + 1) * 128], in_=xnb[b][:, c * 128:(c + 1) * 128],
                            identity=ident[:]),
                            reads=[f"xnb{b}", "ident"], writes=[f"psT{b}"])
                    evac = "act" if t % 2 == 0 else "dve"
                    if evac == "act":
                        k.op("act", lambda e, b=b, t=t: e.copy(
                            out=xnT[:, :, t * 128:(t + 1) * 128],
                            in_=psT[b][:].rearrange("p (c t) -> p c t", c=8)),
                            reads=[f"psT{b}"], writes=[f"xnT{t}"])
                    else:
                        k.op("dve", lambda e, b=b, t=t: e.tensor_copy(
                            out=xnT[:, :, t * 128:(t + 1) * 128],
                            in_=psT[b][:].rearrange("p (c t) -> p c t", c=8)),
                            reads=[f"psT{b}"], writes=[f"xnT{t}"])
                xnT_all = [f"xnT{t}" for t in range(NT)]

                cnt = {"psi": 0, "stgi": 0}

                def blk_ops(cb, wb):
                    psi = cnt["psi"]
                    stgi = cnt["stgi"]
                    col0 = cb * 512
                    k.dma("pool", wbf[wb][:],
                          w_in_d[:, col0:col0 + 512].rearrange("(kc p) c -> p kc c", p=128),
                          writes=[f"wbf{wb}"])
                    if col0 < 1024:
                        kind = ("fm", qaT_d, col0, BF16, None)
                    elif col0 < 2048:
                        kind = ("fm", kaT_d, col0 - 1024, BF16, None)
                    elif col0 < 3072:
                        kind = ("va", va_d, col0 - 2048, BF16, None)
                    elif col0 < 4096:
                        kind = ("tm", za_d, col0 - 3072, F32, AF.Silu)
                    elif col0 < 6144:
                        kind = ("fm", xmT_d, col0 - 4096, F32, None)
                    elif col0 < 8192:
                        kind = ("fm", zmT_d, col0 - 6144, F32, AF.Silu)
                    elif col0 < 10240:
                        kind = ("tm", om_d, col0 - 8192, F32, AF.Sigmoid)
                    else:
                        kind = ("fm", gT_d, col0 - 10240, F32, AF.Sigmoid)
                    lay, dst, c0, odt, func = kind
                    if lay == "fm":
                        for cc in range(4):
                            f0 = c0 + cc * 128
                            sg = stgi % 2
                            stgi += 1
                            st = stg[sg]
                            if odt == BF16:
                                stv = st[:].bitcast(BF16)[:, 0:S]
                            else:
                                stv = st[:]
                            for g in range(8):
                                p = psi % 4
                                psi += 1
                                for kc in range(8):
                                    k.op("pe", lambda e, p=p, wb=wb, kc=kc, cc=cc, g=g: e.matmul(
                                        psM[p][:], lhsT=wbf[wb][:, kc, cc * 128:(cc + 1) * 128],
                                        rhs=xnT[:, kc, g * 512:(g + 1) * 512],
                                        start=(kc == 0), stop=(kc == 7)),
                                        reads=[f"wbf{wb}"] + xnT_all[g * 4:(g + 1) * 4], writes=[f"psM{p}"])
                                osl = stv[:, g * 512:(g + 1) * 512]
                                if func is None and (g % 2 == 0 or not MERGE_MA):
                                    k.op("dve", lambda e, p=p, osl=osl: e.tensor_copy(out=osl, in_=psM[p][:]),
                                         reads=[f"psM{p}"], writes=[f"stg{sg}_{g}"])
                                elif func is None:
                                    k.op("act", lambda e, p=p, osl=osl: e.copy(out=osl, in_=psM[p][:]),
                                         reads=[f"psM{p}"], writes=[f"stg{sg}_{g}"])
                                elif dst is gT_d:
                                    ch = f0 // 128
                                    k.op("act", lambda e, p=p, osl=osl, ch=ch: e.activation(
                                        out=osl, in_=psM[p][:], func=AF.Sigmoid, bias=gateb[:, ch:ch + 1]),
                                        reads=[f"psM{p}", "gateb"], writes=[f"stg{sg}_{g}"])
                                else:
                                    k.op("act", lambda e, p=p, osl=osl, func=func: e.activation(
                                        out=osl, in_=psM[p][:], func=func),
                                        reads=[f"psM{p}"], writes=[f"stg{sg}_{g}"])
                            k.dma("sp", dst[f0:f0 + 128, :], stv, reads=[f"stg{sg}_{g_}" for g_ in range(8)],
                                  writes=([f"xmT_d{f0 // 128}"] if dst is xmT_d else []))
                    else:
                        for tg in range(8):
                            sg = stgi % 2
                            stgi += 1
                            st = stg[sg]
                            if lay == "va":
                                stv = st[:].bitcast(BF16)[:, 0:4 * 8 * 65].rearrange(
                                    "p (t h c) -> p t h c", t=4, h=8)
                                if tg < 2 and cb == 4:
                                    pass
                                k.op("pool", lambda e, stv=stv: e.memset(stv[:, :, :, 64:65], 1.0),
                                     writes=[f"stg{sg}_{g_}" for g_ in range(8)])
                            else:
                                stv = st[:, 0:2048].rearrange("p (t c) -> p t c", t=4)
                            for tt in range(4):
                                t = tg * 4 + tt
                                p = psi % 4
                                psi += 1
                                for kc in range(8):
                                    k.op("pe", lambda e, p=p, wb=wb, kc=kc, t=t: e.matmul(
                                        psM[p][:], lhsT=xnT[:, kc, t * 128:(t + 1) * 128],
                                        rhs=wbf[wb][:, kc, :], start=(kc == 0), stop=(kc == 7)),
                                        reads=[f"wbf{wb}", f"xnT{t}"], writes=[f"psM{p}"])
                                if lay == "va":
                                    k.op("dve", lambda e, p=p, stv=stv, tt=tt: e.tensor_copy(
                                        out=stv[:, tt, :, 0:64],
                                        in_=psM[p][:].rearrange("p (h c) -> p h c", h=8)),
                                        reads=[f"psM{p}"], writes=[f"stg{sg}_{tt}"])
                                else:
                                    k.op("act", lambda e, p=p, stv=stv, tt=tt, func=func: e.activation(
                                        out=stv[:, tt, :], in_=psM[p][:], func=func),
                                        reads=[f"psM{p}"], writes=[f"stg{sg}_{tt}"])
                            r0 = tg * 512
                            if lay == "va":
                                h0 = c0 // 64
                                k.dma("sp", dst[r0:r0 + 512, h0:h0 + 8, :].rearrange("(t p) h c -> p t h c", p=128),
                                      stv, reads=[f"stg{sg}_{g_}" for g_ in range(8)])
                            else:
                                k.dma("sp", dst[r0:r0 + 512, c0:c0 + 512].rearrange("(t p) c -> p t c", p=128),
                                      stv, reads=[f"stg{sg}_{g_}" for g_ in range(8)])
                    cnt["psi"] = psi
                    cnt["stgi"] = stgi

                ORDER = [8, 9, 10, 11, 0, 1, 2, 3, 4, 5, 6, 7, 12, 13, 14, 15, 16, 17, 18, 19, 20, 21, 22, 23]
                if not MERGE_MA:
                    for i_, cb in enumerate(range(24)):
                        blk_ops(cb, i_ % 2)
                else:
                    cw = sb("ma_cw", [128, 16, 4], F32)
                    cb = sb("ma_cb", [128, 16], F32)
                    bif = sb("ma_bif", [8, 1], F32)
                    bdf = sb("ma_bdf", [128, 16, 128], F32)
                    bd = [sb(f"ma_bd{i}", [128, 16, 128], BF16) for i in range(3)]
                    wiff = sb("ma_wiff", [128, 48, 8], F32)
                    wif = sb("ma_wif", [128, 48, 8], BF16)
                    xm = [sb(f"ma_xm{i}", [128, 515], F32) for i in range(2)]
                    acc = [sb(f"ma_acc{i}", [128, 512], F32) for i in range(2)]
                    tmpc = [sb(f"ma_tmpc{i}", [128, 512], F32) for i in range(2)]
                    xcf = [sb(f"ma_xcf{i}", [128, 512], F32) for i in range(2)]
                    xcb = [sb(f"ma_xcb{i}", [128, 512], BF16) for i in range(2)]
                    sxf = [sb(f"ma_sxf{i}", [128, 512], F32) for i in range(2)]
                    skp = sb("ma_skp", [128, 16], F32)
                    xmb = [sb(f"ma_xmb{i}", [128, 512], BF16) for i in range(2)]
                    qTb = [sb(f"ma_qTb{i}", [128, 512], BF16) for i in range(3)]
                    kTb = [sb(f"ma_kTb{i}", [128, 512], BF16) for i in range(3)]
                    ksb = [sb(f"ma_ksb{i}", [128, 512], BF16) for i in range(2)]
                    vTb = [sb(f"ma_vTb{i}", [128, 512], BF16) for i in range(3)]
                    gsb = [sb(f"ma_gsb{i}", [8, 512], F32) for i in range(2)]
                    psq = [psT[0][:].bitcast(F32)] * 2
                    psk = [psT[1][:].bitcast(F32)] * 2
                    psv_t = ps("mav", [128, 512])
                    psv = [psv_t[:]] * 2
                    psg_t = ps("mag", [128, 512])
                    psg = [psg_t[:]] * 2

                    k.dma("sp", cw[:], cw_d, writes=["cw"])
                    k.dma("sp", cb[:], cb_d, writes=["cb"])
                    k.dma("sp", skp[:], skip_d, writes=["skp"])
                    k.dma("sp", bif[:], bif_d, writes=["bif"])
                    k.dma("sp", wiff[:], wif_d, writes=["wiff"])
                    k.op("dve", lambda e: e.tensor_copy(out=wif[:], in_=wiff[:]), reads=["wiff"], writes=["wif"])
                    for i, bdd in enumerate((bdq_d, bdk_d, bdv_d)):
                        k.dma("sp", bdf[:], bdd, writes=["bdf"])
                        k.op("dve", lambda e, i=i: e.tensor_copy(out=bd[i][:], in_=bdf[:]),
                             reads=["bdf"], writes=[f"bd{i}"])
                    KS = float(512 ** -0.5)

                    def load_xm(it_):
                        g_, c_ = it_ // 16, it_ % 16
                        b_ = it_ % 2
                        rows_ = slice(c_ * 128, (c_ + 1) * 128)
                        if g_ == 0:
                            k.op("pool", lambda e: e.memset(xm[b_][:, 0:3], 0.0), writes=[f"xm{b_}"])
                            k.dma("sp", xm[b_][:, 3:515], xmT_d[rows_, 0:512], reads=[f"xmT_d{c_}"], writes=[f"xm{b_}"])
                        else:
                            k.dma("sp", xm[b_][:], xmT_d[rows_, g_ * 512 - 3:g_ * 512 + 512], reads=[f"xmT_d{c_}"], writes=[f"xm{b_}"])


                    def ma_x(it):
                        g, c = it // 16, it % 16
                        b = it % 2
                        t0 = g * 512
                        rows = slice(c * 128, (c + 1) * 128)
                        k.op("act", lambda e: e.activation(
                            out=acc[b][:], in_=xm[b][:, 0:512], func=AF.Identity, scale=cw[:, c, 0:1]),
                            reads=[f"xm{b}", "cw"], writes=[f"acc{b}"])
                        for tap in range(1, 4):
                            k.op("dve", lambda e, tap=tap: e.scalar_tensor_tensor(
                                out=acc[b][:], in0=xm[b][:, tap:tap + 512], scalar=cw[:, c, tap:tap + 1],
                                in1=acc[b][:], op0=ALU.mult, op1=ALU.add),
                                reads=[f"xm{b}", "cw", f"acc{b}"], writes=[f"acc{b}"])
                        k.op("act", lambda e: e.activation(
                            out=xcf[b][:], in_=acc[b][:], func=AF.Silu, bias=cb[:, c:c + 1]),
                            reads=[f"acc{b}", "cb"], writes=[f"xcf{b}"])
                        k.op("dve", lambda e: e.tensor_copy(out=xcb[b][:], in_=xcf[b][:]),
                             reads=[f"xcf{b}"], writes=[f"xcb{b}"])
                        k.op("dve", lambda e: e.tensor_copy(out=xmb[b][:], in_=xm[b][:, 3:515]),
                             reads=[f"xm{b}"], writes=[f"xmb{b}"])
                        k.op("dve", lambda e: e.tensor_scalar(
                            out=sxf[b][:], in0=xcf[b][:], scalar1=skp[:, c:c + 1], scalar2=None, op0=ALU.mult),
                            reads=[f"xcf{b}", "skp"], writes=[f"sxf{b}"])
                        k.dma("sp", sxT_d[rows, t0:t0 + 512], sxf[b][:], reads=[f"sxf{b}"])
                        k.dma("sp", xcbT_d[rows, t0:t0 + 512], xcb[b][:], reads=[f"xcb{b}"])
                        k.dma("sp", xmbT_d[rows, t0:t0 + 512], xmb[b][:], reads=[f"xmb{b}"])

                    def ma_y(it):
                        g, c = it // 16, it % 16
                        b = it % 2
                        b3 = it % 3
                        t0 = g * 512
                        rows = slice(c * 128, (c + 1) * 128)
                        gp = psg[g % 2]
                        gpn = "psgm"
                        k.op("pe", lambda e: e.matmul(psq[b], lhsT=bd[0][:, c, :], rhs=xcb[b][:], start=True, stop=True),
                             reads=["bd0", f"xcb{b}"], writes=["psT0"])
                        k.op("pe", lambda e: e.matmul(psk[b], lhsT=bd[1][:, c, :], rhs=xcb[b][:], start=True, stop=True),
                             reads=["bd1", f"xcb{b}"], writes=["psT1"])
                        k.op("pe", lambda e: e.matmul(psv[b], lhsT=bd[2][:, c, :], rhs=xmb[b][:], start=True, stop=True),
                             reads=["bd2", f"xmb{b}"], writes=["psvm"])
                        k.op("dve", lambda e: e.tensor_copy(out=qTb[b3][:], in_=psq[b]),
                             reads=["psT0"], writes=[f"qTb{b3}"])
                        k.op("act", lambda e: e.copy(out=kTb[b3][:], in_=psk[b]),
                             reads=["psT1"], writes=[f"kTb{b3}"])
                        k.op("act", lambda e: e.activation(out=ksb[b][:], in_=psk[b], func=AF.Identity, scale=KS),
                             reads=["psT1"], writes=[f"ksb{b}"])
                        k.op("act", lambda e: e.copy(out=vTb[b3][:], in_=psv[b]),
                             reads=["psvm"], writes=[f"vTb{b3}"])
                        k.dma("sp", qmT_d[rows, t0:t0 + 512], qTb[b3][:], reads=[f"qTb{b3}"])
                        k.dma("sp", ksT_d[rows, t0:t0 + 512], ksb[b][:], reads=[f"ksb{b}"])

                    def ma_z(it):
                        g, c = it // 16, it % 16
                        b3 = it % 3
                        t0 = g * 512
                        gp = psg[g % 2]
                        gpn = "psgm"
                        for j, (src, sn) in enumerate(((qTb, "qTb"), (kTb, "kTb"), (vTb, "vTb"))):
                            k.op("pe", lambda e, j=j, src=src: e.matmul(
                                gp[0:8, :], lhsT=wif[:, j * 16 + c, :], rhs=src[b3][:],
                                start=(c == 0 and j == 0), stop=(c == 15 and j == 2)),
                                reads=["wif", f"{sn}{b3}"], writes=[gpn])
                        if c == 15:
                            gs = gsb[g % 2]
                            k.op("act", lambda e: e.activation(
                                out=gs[:], in_=gp[0:8, :], func=AF.Identity, bias=bif[:, 0:1]),
                                reads=[gpn, "bif"], writes=[f"gsb{g % 2}"])
                            k.dma("sp", gpre_d[:, t0:t0 + 512], gs[:], reads=[f"gsb{g % 2}"])

                    def ma_iter_list(it):
                        ls = []
                        if it < 128:
                            def x_with_load(it=it):
                                if it + 1 < 128:
                                    load_xm(it + 1)
                                ma_x(it)
                            ls.append(k.collect(x_with_load))
                        if 1 <= it <= 128:
                            ls.append(k.collect(ma_y, it - 1))
                        if it >= 2:
                            ls.append(k.collect(ma_z, it - 2))
                        return k.merge_lists(ls)

                    for i_ in range(4):
                        blk_ops(ORDER[i_], i_ % 2)
                    load_xm(0)
                    it_next = 0
                    NOV = 12
                    for i_ in range(4, 4 + NOV):
                        n_it = (130 - it_next + (4 + NOV - i_) - 1) // (4 + NOV - i_)
                        ma_l = []
                        for it in range(it_next, it_next + n_it):
                            ma_l += ma_iter_list(it)
                        it_next += n_it
                        bl = k.collect(blk_ops, ORDER[i_], i_ % 2)
                        k.emit_interleaved([bl, ma_l])
                    assert it_next == 130
                    for i_ in range(4 + NOV, 24):
                        blk_ops(ORDER[i_], i_ % 2)
                k.barrier()
                k.flush()
        if "p2" in phases:
            with ExitStack() as ph:
                def sb(name, shape, dt):
                    return ph.enter_context(nc.sbuf_tensor("sb2_" + name, list(shape), dt))

                def ps(name, shape, dt=F32):
                    return ph.enter_context(nc.psum_tensor("ps2_" + name, list(shape), dt))

                qT = [sb(f"qT{i}", [128, S], BF16) for i in range(2)]
                kTe = [sb(f"kTe{i}", [128, S], BF16) for i in range(2)]
                kTo = [sb(f"kTo{i}", [128, S], BF16) for i in range(2)]
                Eall = [sb(f"E{p}", [128, 16, 2, 128], F32) for p in range(3)]
                mask = sb("mask", [128, 2, 128], F32)
                vt = [sb(f"vt{i}", [128, 32, 2, 65], BF16) for i in range(2)]
                pex = [sb(f"pex{i}", [128, 2, 2, 128], F32) for i in range(4)]
                pt = [sb(f"pt{i}", [128, 2, 2, 128], BF16) for i in range(6)]
                ost = [sb(f"ost{i}", [128, 3, 130], F32) for i in range(3)]
                psL = [ps(f"psL{i}", [128, 2, 2, 128], F32) for i in range(4)]
                psO = [ps(f"psO{i}", [128, 512], F32) for i in range(2)]
                PATS = [(p, d) for p, d in enumerate((1, 4, 16))
                        if str(p) in os.environ.get('ATT_PATTERNS', '012')]
                NHP = int(os.environ.get("ATT_NHP", "8"))
                for i in range(2):
                    k.op("pool", lambda e, i=i: e.memset(kTe[i][64:128, :], 0.0), writes=[f"kTe{i}"])
                    k.op("pool", lambda e, i=i: e.memset(kTo[i][0:64, :], 0.0), writes=[f"kTo{i}"])
                k.dma("sp", mask[:], mask_d, writes=["mask"])

                def qk_load(hp):
                    sl = hp % 2
                    k.dma("sp", qT[sl][:], qaT_d[hp * 128:(hp + 1) * 128, :], writes=[f"qT{sl}"])
                    k.dma("sp", kTe[sl][0:64, :], kaT_d[hp * 128:hp * 128 + 64, :], writes=[f"kTe{sl}"])
                    k.dma("sp", kTo[sl][64:128, :], kaT_d[hp * 128 + 64:(hp + 1) * 128, :], writes=[f"kTo{sl}"])

                groups = [(hp, p, d) for hp in range(NHP) for (p, d) in PATS]

                def v_load(gi):
                    hp, p, d = groups[gi]
                    nb = 32 // d
                    sl = gi % 2
                    for r in range(d):
                        src = va_d[r:r + (S // d - 1) * d + 1:d, 2 * hp:2 * hp + 2, :].rearrange(
                            "(n j) h c -> j n h c", j=128)
                        k.dma("sp", vt[sl][:, r * nb:(r + 1) * nb, :, :], src, writes=[f"vt{sl}"])

                qk_load(0)
                v_load(0)
                for p, d in PATS:
                    E = Eall[p]
                    en = f"E{p}"
                    enh = [f"E{p}_{h}" for h in range(16)]
                    k.dma("sp", E[:], bias_d[p], writes=enh)
                    k.op("act", lambda e, E=E: e.activation(out=E[:], in_=E[:], func=AF.Exp),
                         reads=enh, writes=enh)
                    for h in range(16):
                        k.op("dve", lambda e, E=E, h=h: e.tensor_tensor(
                            out=E[:, h, :, :], in0=E[:, h, :, :], in1=mask[:], op=ALU.mult),
                            reads=[enh[h], "mask"], writes=[enh[h]])
                LAG = 3
                units = []
                for gi, (hp, p, d) in enumerate(groups):
                    nb = 32 // d
                    for bi in range(32):
                        units.append((gi, hp, p, d, bi, bi // nb, bi % nb))
                pend = []
                bc = 0

                def flush_o(bank, items):
                    ot = ost[bank % 3]
                    otn = f"ost{bank % 3}"
                    n_it = len(items)
                    pso = psO[bank % 2]
                    if True:
                        k.op("act", lambda e: e.copy(
                            out=ot[:, 0:n_it, :], in_=pso[:, 0:130 * n_it].rearrange("q (j c) -> q j c", j=n_it)),
                            reads=[f"psO{bank % 2}"], writes=[otn])
                    else:
                        k.op("dve", lambda e: e.tensor_copy(
                            out=ot[:, 0:n_it, :], in_=pso[:, 0:130 * n_it].rearrange("q (j c) -> q j c", j=n_it)),
                            reads=[f"psO{bank % 2}"], writes=[otn])
                    for j, (p_, hp_, rs) in enumerate(items):
                        k.dma("pool", att_d[p_][rs, 2 * hp_:2 * hp_ + 2, :],
                              ot[:, j, :].rearrange("q (h c) -> q h c", h=2), reads=[otn])

                for i in range(len(units) + LAG):
                    if i < len(units):
                        gi, hp, p, d, bi, r, n = units[i]
                        qs_ = hp % 2
                        q0 = r + d * 128 * n
                        qsl = slice(q0, q0 + 127 * d + 1, d)
                        ksl = [slice(q0 - 128 * d, q0 - 128 * d + 127 * d + 1, d) if n > 0 else None, qsl]
                        cs = [1] if n == 0 else [0, 1]
                        lsl = i % 4
                        L = psL[lsl]
                        ln = f"psL{lsl}"
                        for hh in range(2):
                            kz = kTe[qs_] if hh == 0 else kTo[qs_]
                            kzn = (f"kTe{qs_}" if hh == 0 else f"kTo{qs_}")
                            for c in cs:
                                k.op("pe", lambda e, L=L, c=c, hh=hh, kz=kz, qs_=qs_, ks=ksl[c], qs=qsl: e.matmul(
                                    L[:, hh, c, :], lhsT=kz[:, ks], rhs=qT[qs_][:, qs], start=True, stop=True),
                                    reads=[kzn, f"qT{qs_}"], writes=[ln])
                        c0 = cs[0]
                        px = pex[i % 4]
                        pxn = f"pex{i % 4}"
                        k.op("act", lambda e, L=L, px=px, c0=c0: e.activation(
                            out=px[:, :, c0:2, :], in_=L[:, :, c0:2, :], func=AF.Exp, scale=0.125),
                            reads=[ln], writes=[pxn])
                        ptt = pt[i % 6]
                        ptn = f"pt{i % 6}"
                        E = Eall[p]
                        k.op("dve", lambda e, px=px, ptt=ptt, E=E, hp=hp, c0=c0: e.tensor_tensor(
                            out=ptt[:, :, c0:2, :], in0=px[:, :, c0:2, :], in1=E[:, 2 * hp:2 * hp + 2, c0:2, :],
                            op=ALU.mult),
                            reads=[pxn, f"E{p}_{2 * hp}", f"E{p}_{2 * hp + 1}"], writes=[ptn])
                        if bi == LAG + 2:
                            if gi + 1 < len(groups):
                                v_load(gi + 1)
                                if groups[gi + 1][0] != hp:
                                    qk_load(groups[gi + 1][0])
                    j = i - LAG
                    if j >= 0:
                        gi, hp, p, d, bi, r, n = units[j]
                        cs = [1] if n == 0 else [0, 1]
                        ptt = pt[j % 6]
                        ptn = f"pt{j % 6}"
                        bank = bc // 3
                        j3 = bc % 3
                        bc += 1
                        vs = gi % 2
                        for hh in range(2):
                            O = psO[bank % 2][:, j3 * 130 + hh * 65:j3 * 130 + hh * 65 + 65]
                            for c in cs:
                                k.op("pe", lambda e, O=O, ptt=ptt, c=c, vs=vs, hh=hh, bi=bi, cs=cs: e.matmul(
                                    O, lhsT=ptt[:, hh, c, :], rhs=vt[vs][:, bi - 1 + c, hh, :],
                                    start=(c == cs[0]), stop=(c == 1)),
                                    reads=[ptn, f"vt{vs}"], writes=[f"psO{bank % 2}"])
                        st0 = r + d * 128 * n
                        pend.append((p, hp, slice(st0, st0 + 127 * d + 1, d)))
                        if j3 == 2 or j == len(units) - 1:
                            flush_o(bank, pend)
                            pend = []
                k.barrier()
                k.flush()
        if "ma" in phases and not MERGE_MA:
            with ExitStack() as ph:
                def sb(name, shape, dt):
                    return ph.enter_context(nc.sbuf_tensor("sb3_" + name, list(shape), dt))

                def ps(name, shape, dt=F32):
                    return ph.enter_context(nc.psum_tensor("ps3_" + name, list(shape), dt))

                cw = sb("cw", [128, 16, 4], F32)
                cb = sb("cb", [128, 16], F32)
                bif = sb("bif", [8, 1], F32)
                bdf = sb("bdf", [128, 16, 128], F32)
                bd = [sb(f"bd{i}", [128, 16, 128], BF16) for i in range(3)]
                wiff = sb("wiff", [128, 48, 8], F32)
                wif = sb("wif", [128, 48, 8], BF16)
                xm = [sb(f"xm{i}", [128, 515], F32) for i in range(2)]
                acc = [sb(f"acc{i}", [128, 512], F32) for i in range(2)]
                tmpc = [sb(f"tmpc{i}", [128, 512], F32) for i in range(2)]
                xcf = [sb(f"xcf{i}", [128, 512], F32) for i in range(2)]
                xcb = [sb(f"xcb{i}", [128, 512], BF16) for i in range(2)]
                sxf = [sb(f"sxf{i}", [128, 512], F32) for i in range(2)]
                skp = sb("skp", [128, 16], F32)
                xmb = [sb(f"xmb{i}", [128, 512], BF16) for i in range(2)]
                qTb = [sb(f"qTb{i}", [128, 512], BF16) for i in range(2)]
                kTb = [sb(f"kTb{i}", [128, 512], BF16) for i in range(2)]
                ksb = [sb(f"ksb{i}", [128, 512], BF16) for i in range(2)]
                vTb = [sb(f"vTb{i}", [128, 512], BF16) for i in range(2)]
                gsb = [sb(f"gsb{i}", [8, 512], F32) for i in range(2)]
                psq = [ps(f"q{i}", [128, 512]) for i in range(2)]
                psk = [ps(f"k{i}", [128, 512]) for i in range(2)]
                psv = [ps(f"v{i}", [128, 512]) for i in range(2)]
                psg = [ps(f"g{i}", [128, 512]) for i in range(2)]

                k.dma("sp", cw[:], cw_d, writes=["cw"])
                k.dma("sp", cb[:], cb_d, writes=["cb"])
                k.dma("sp", skp[:], skip_d, writes=["skp"])
                k.dma("sp", bif[:], bif_d, writes=["bif"])
                k.dma("sp", wiff[:], wif_d, writes=["wiff"])
                k.op("dve", lambda e: e.tensor_copy(out=wif[:], in_=wiff[:]), reads=["wiff"], writes=["wif"])
                for i, bdd in enumerate((bdq_d, bdk_d, bdv_d)):
                    k.dma("sp", bdf[:], bdd, writes=["bdf"])
                    k.op("dve", lambda e, i=i: e.tensor_copy(out=bd[i][:], in_=bdf[:]),
                         reads=["bdf"], writes=[f"bd{i}"])
                KS = float(512 ** -0.5)

                def load_xm(it_):
                    g_, c_ = it_ // 16, it_ % 16
                    b_ = it_ % 2
                    rows_ = slice(c_ * 128, (c_ + 1) * 128)
                    if g_ == 0:
                        k.op("pool", lambda e: e.memset(xm[b_][:, 0:3], 0.0), writes=[f"xm{b_}"])
                        k.dma("sp", xm[b_][:, 3:515], xmT_d[rows_, 0:512], writes=[f"xm{b_}"])
                    else:
                        k.dma("sp", xm[b_][:], xmT_d[rows_, g_ * 512 - 3:g_ * 512 + 512], writes=[f"xm{b_}"])

                load_xm(0)

                def ma_x(it):
                    g, c = it // 16, it % 16
                    b = it % 2
                    t0 = g * 512
                    rows = slice(c * 128, (c + 1) * 128)
                    k.op("act", lambda e: e.activation(
                        out=acc[b][:], in_=xm[b][:, 0:512], func=AF.Identity, scale=cw[:, c, 0:1]),
                        reads=[f"xm{b}", "cw"], writes=[f"acc{b}"])
                    for tap in range(1, 4):
                        k.op("dve", lambda e, tap=tap: e.scalar_tensor_tensor(
                            out=acc[b][:], in0=xm[b][:, tap:tap + 512], scalar=cw[:, c, tap:tap + 1],
                            in1=acc[b][:], op0=ALU.mult, op1=ALU.add),
                            reads=[f"xm{b}", "cw", f"acc{b}"], writes=[f"acc{b}"])
                    k.op("act", lambda e: e.activation(
                        out=xcf[b][:], in_=acc[b][:], func=AF.Silu, bias=cb[:, c:c + 1]),
                        reads=[f"acc{b}", "cb"], writes=[f"xcf{b}"])
                    k.op("dve", lambda e: e.tensor_copy(out=xcb[b][:], in_=xcf[b][:]),
                         reads=[f"xcf{b}"], writes=[f"xcb{b}"])
                    k.op("dve", lambda e: e.tensor_copy(out=xmb[b][:], in_=xm[b][:, 3:515]),
                         reads=[f"xm{b}"], writes=[f"xmb{b}"])
                    k.op("dve", lambda e: e.tensor_scalar(
                        out=sxf[b][:], in0=xcf[b][:], scalar1=skp[:, c:c + 1], scalar2=None, op0=ALU.mult),
                        reads=[f"xcf{b}", "skp"], writes=[f"sxf{b}"])
                    k.dma("sp", sxT_d[rows, t0:t0 + 512], sxf[b][:], reads=[f"sxf{b}"])
                    k.dma("sp", xcbT_d[rows, t0:t0 + 512], xcb[b][:], reads=[f"xcb{b}"])
                    k.dma("sp", xmbT_d[rows, t0:t0 + 512], xmb[b][:], reads=[f"xmb{b}"])

                def ma_y(it):
                    g, c = it // 16, it % 16
                    b = it % 2
                    t0 = g * 512
                    rows = slice(c * 128, (c + 1) * 128)
                    gp = psg[g % 2]
                    gpn = f"psg{g % 2}"
                    k.op("pe", lambda e: e.matmul(psq[b][:], lhsT=bd[0][:, c, :], rhs=xcb[b][:], start=True, stop=True),
                         reads=["bd0", f"xcb{b}"], writes=[f"psq{b}"])
                    k.op("pe", lambda e: e.matmul(psk[b][:], lhsT=bd[1][:, c, :], rhs=xcb[b][:], start=True, stop=True),
                         reads=["bd1", f"xcb{b}"], writes=[f"psk{b}"])
                    k.op("pe", lambda e: e.matmul(psv[b][:], lhsT=bd[2][:, c, :], rhs=xmb[b][:], start=True, stop=True),
                         reads=["bd2", f"xmb{b}"], writes=[f"psv{b}"])
                    k.op("dve", lambda e: e.tensor_copy(out=qTb[b][:], in_=psq[b][:]),
                         reads=[f"psq{b}"], writes=[f"qTb{b}"])
                    k.op("act", lambda e: e.copy(out=kTb[b][:], in_=psk[b][:]),
                         reads=[f"psk{b}"], writes=[f"kTb{b}"])
                    k.op("act", lambda e: e.activation(out=ksb[b][:], in_=psk[b][:], func=AF.Identity, scale=KS),
                         reads=[f"psk{b}"], writes=[f"ksb{b}"])
                    k.op("act", lambda e: e.copy(out=vTb[b][:], in_=psv[b][:]),
                         reads=[f"psv{b}"], writes=[f"vTb{b}"])
                    k.dma("sp", qmT_d[rows, t0:t0 + 512], qTb[b][:], reads=[f"qTb{b}"])
                    k.dma("sp", ksT_d[rows, t0:t0 + 512], ksb[b][:], reads=[f"ksb{b}"])
                    for j, (src, sn) in enumerate(((qTb, "qTb"), (kTb, "kTb"), (vTb, "vTb"))):
                        k.op("pe", lambda e, j=j, src=src: e.matmul(
                            gp[0:8, :], lhsT=wif[:, j * 16 + c, :], rhs=src[b][:],
                            start=(c == 0 and j == 0), stop=(c == 15 and j == 2)),
                            reads=["wif", f"{sn}{b}"], writes=[gpn])
                    if c == 15:
                        gs = gsb[g % 2]
                        k.op("act", lambda e: e.activation(
                            out=gs[:], in_=gp[0:8, :], func=AF.Identity, bias=bif[:, 0:1]),
                            reads=[gpn, "bif"], writes=[f"gsb{g % 2}"])
                        k.dma("sp", gpre_d[:, t0:t0 + 512], gs[:], reads=[f"gsb{g % 2}"])

                for it in range(129):
                    lists = []
                    if it < 128:
                        if it + 1 < 128:
                            load_xm(it + 1)
                        lists.append(k.collect(ma_x, it))
                    if it >= 1:
                        lists.append(k.collect(ma_y, it - 1))
                    k.emit_interleaved(lists)
                k.barrier()
                k.flush()
        if "mb" in phases:
            with ExitStack() as ph:
                def sb(name, shape, dt):
                    return ph.enter_context(nc.sbuf_tensor("sb4_" + name, list(shape), dt))

                def ps(name, shape, dt=F32):
                    return ph.enter_context(nc.psum_tensor("ps4_" + name, list(shape), dt))

                li = sb("li", [4, S], F32)
                fp = sb("fp", [4, S], F32)
                e1 = sb("e1", [4, S], F32)
                ll = sb("ll", [4, S], F32)
                ones = sb("ones", [4, S], F32)
                Bn = sb("Bn", [4, S], F32)
                cg = sb("cg", [4, S], F32)
                Mg = sb("Mg", [4, S], F32)
                qu = sb("qu", [4, S], F32)
                qw = sb("qw", [4, S], F32)
                qe = sb("qe", [4, S], F32)
                qs = sb("qs", [4, S], F32)
                rr = sb("rr", [4, 32], F32)
                nr = sb("nr", [4, 32], F32)
                nre = sb("nre", [4, 32], F32)
                idf = sb("idf", [128, 128], F32)
                sel = sb("sel", [128, 128], F32)
                G = sb("G", [128, 32, 16], F32)
                Gl = sb("Gl", [128, 32, 16], F32)
                psG = ps("G", [128, 32, 16])
                psGl = ps("Gl", [128, 512])
                k.dma("sp", li[:], gpre_d[0:4, :], writes=["li"])
                k.dma("sp", fp[:], gpre_d[4:8, :], writes=["fp"])
                k.dma("sp", idf[:], ident_d, writes=["idf"])
                k.dma("sp", sel[:], sel_d, writes=["sel"])
                k.op("act", lambda e: e.activation(out=e1[:], in_=fp[:], func=AF.Exp, scale=-1.0),
                     reads=["fp"], writes=["e1"])
                k.op("act", lambda e: e.activation(out=ll[:], in_=e1[:], func=AF.Ln, bias=1.0),
                     reads=["e1"], writes=["ll"])
                k.op("pool", lambda e: e.memset(ones[:], 1.0), writes=["ones"])
                k.op("dve", lambda e: e.tensor_tensor_scan(out=Bn[:], data0=ones[:], data1=ll[:], initial=0.0,
                                                           op0=ALU.mult, op1=ALU.add),
                     reads=["ones", "ll"], writes=["Bn"])
                k.op("dve", lambda e: e.tensor_tensor(out=cg[:], in0=li[:], in1=Bn[:], op=ALU.add),
                     reads=["li", "Bn"], writes=["cg"])
                k.op("dve", lambda e: e.tensor_tensor_scan(out=Mg[:], data0=cg[:], data1=cg[:], initial=0.0,
                                                           op0=ALU.max, op1=ALU.max),
                     reads=["cg"], writes=["Mg"])
                Mgv = Mg[:].rearrange("p (c j) -> p c j", j=128)
                k.op("pool", lambda e: e.memset(rr[:, 0:1], 0.0), writes=["rr"])
                k.op("dve", lambda e: e.tensor_copy(out=rr[:, 1:32], in_=Mgv[:, 0:31, 127]),
                     reads=["Mg"], writes=["rr"])
                k.op("dve", lambda e: e.tensor_scalar(out=nr[:], in0=rr[:], scalar1=-1.0, scalar2=None, op0=ALU.mult),
                     reads=["rr"], writes=["nr"])
                k.op("dve", lambda e: e.tensor_scalar(out=nre[:], in0=Mgv[:, :, 127], scalar1=-1.0, scalar2=None,
                                                      op0=ALU.mult),
                     reads=["Mg"], writes=["nre"])
                k.op("dve", lambda e: e.tensor_tensor(out=qe[:], in0=Bn[:], in1=Mg[:], op=ALU.subtract),
                     reads=["Bn", "Mg"], writes=["qe"])
                k.op("act", lambda e: e.activation(out=qe[:], in_=qe[:], func=AF.Exp), reads=["qe"], writes=["qe"])
                for c in range(32):
                    sl = slice(c * 128, (c + 1) * 128)
                    k.op("act", lambda e, sl=sl, c=c: e.activation(out=qu[:, sl], in_=cg[:, sl], func=AF.Exp,
                                                                  bias=nr[:, c:c + 1]),
                         reads=["cg", "nr"], writes=[f"qu{c}"])
                    k.op("act", lambda e, sl=sl, c=c: e.activation(out=qw[:, sl], in_=Mg[:, sl], func=AF.Exp,
                                                                  scale=-1.0, bias=rr[:, c:c + 1]),
                         reads=["Mg", "rr"], writes=[f"qw{c}"])
                    k.op("act", lambda e, sl=sl, c=c: e.activation(out=qs[:, sl], in_=cg[:, sl], func=AF.Exp,
                                                                  bias=nre[:, c:c + 1]),
                         reads=["cg", "nre"], writes=[f"qs{c}"])
                for c in range(32):
                    sl = slice(c * 128, (c + 1) * 128)
                    for i, (X, xn_) in enumerate(((qu, "qu"), (qw, "qw"), (qe, "qe"), (qs, "qs"))):
                        k.op("pe", lambda e, X=X, sl=sl, c=c, i=i: e.transpose(
                            out=psG[:, c, 4 * i:4 * i + 4], in_=X[:, sl], identity=idf[0:4, 0:4]),
                            reads=[(xn_ if xn_ == "qe" else f"{xn_}{c}"), "idf"], writes=["psG"])
                k.op("dve", lambda e: e.tensor_copy(out=G[:], in_=psG[:]), reads=["psG"], writes=["G"])
                k.op("pe", lambda e: e.matmul(psGl[:], lhsT=sel[:], rhs=G[:].rearrange("p c i -> p (c i)"),
                                              start=True, stop=True),
                     reads=["sel", "G"], writes=["psGl"])
                k.op("dve", lambda e: e.tensor_copy(out=Gl[:].rearrange("p c i -> p (c i)"), in_=psGl[:]),
                     reads=["psGl"], writes=["Gl"])
                k.dma("sp", gtm_d, G[:], reads=["G"])
                k.dma("sp", glast_d, Gl[:], reads=["Gl"])
                k.barrier()
                k.flush()
        if "mc" in phases:
            with ExitStack() as ph:
                def sb(name, shape, dt):
                    return ph.enter_context(nc.sbuf_tensor("sb5_" + name, list(shape), dt))

                def ps(name, shape, dt=F32):
                    return ph.enter_context(nc.psum_tensor("ps5_" + name, list(shape), dt))

                G = sb("G", [128, 32, 16], F32)
                Gl = sb("Gl", [128, 32, 16], F32)
                hng = sb("hng", [128, 2048], F32)
                tri = sb("tri", [128, 128], F32)
                idf = sb("idf", [128, 128], F32)
                zero = sb("zero", [128, 1], F32)
                epsc = sb("epsc", [128, 1], F32)
                bdf = sb("bdf", [128, 16, 128], F32)
                bdk = sb("bdk", [128, 16, 128], BF16)
                bdv = sb("bdv", [128, 16, 128], BF16)
                Cf = [sb(f"Cf{h}", [128, 4, 512], F32) for h in range(4)]
                Cb = [sb(f"Cb{h}", [128, 4, 512], BF16) for h in range(4)]
                nf = sb("nf", [128, 4, 4], F32)
                nbf = sb("nbf", [128, 4, 4], BF16)
                qT = [sb(f"qT{i}", [128, 16, 128], BF16) for i in range(2)]
                ksT = [sb(f"ksT{i}", [128, 16, 128], BF16) for i in range(2)]
                xcb = [sb(f"xcb{i}", [128, 16, 128], BF16) for i in range(2)]
                xmb = [sb(f"xmb{i}", [128, 16, 128], BF16) for i in range(2)]
                om = [sb(f"om{i}", [128, 2048], F32) for i in range(2)]
                zmT = [sb(f"zmT{i}", [128, 16, 128], F32) for i in range(2)]
                sxT = [sb(f"sxT{i}", [128, 16, 128], F32) for i in range(2)]
                ucol = [sb(f"ucol{i}", [128, 8], BF16) for i in range(2)]
                ktm = [sb(f"ktm{i}", [128, 512], BF16) for i in range(3)]
                v1 = [sb(f"v1{i}", [128, 512], BF16) for i in range(3)]
                v2 = [sb(f"v2{i}", [128, 512], BF16) for i in range(3)]
                Pm = [sb(f"Pm{i}", [128, 128], BF16) for i in range(3)]
                hs = [sb(f"hs{i}", [128, 512], F32) for i in range(3)]
                hn = [sb(f"hn{i}", [128, 512], F32) for i in range(3)]
                hg = [sb(f"hg{i}", [128, 512], F32) for i in range(3)]
                tt = [sb(f"tt{i}", [128, 4, 128], F32) for i in range(3)]
                hzs = [sb(f"hzs{i}", [128, 16, 128], BF16) for i in range(2)]
                sm = [sb(f"sm{i}", [128, 16], F32) for i in range(3)]
                pk = ps("k", [128, 512])
                pv = ps("v", [128, 512])
                pN = [ps(f"N{i}", [128, 512]) for i in range(2)]
                pkv = [ps(f"kv{i}", [128, 512]) for i in range(2)]
                pT = ps("T", [128, 4, 128])
                pm = ps("m", [128, 512])

                k.dma("sp", G[:], gtm_d, writes=["G"])
                k.dma("sp", Gl[:], glast_d, writes=["Gl"])
                k.dma("sp", hng[:], hng_d, writes=["hng"])
                k.dma("sp", tri[:], tri_d, writes=["tri"])
                k.dma("sp", idf[:], ident_d, writes=["idf"])
                k.dma("sp", bdf[:], bdk_d, writes=["bdf"])
                k.op("dve", lambda e: e.tensor_copy(out=bdk[:], in_=bdf[:]), reads=["bdf"], writes=["bdk"])
                k.dma("sp", bdf[:], bdv_d, writes=["bdf"])
                k.op("dve", lambda e: e.tensor_copy(out=bdv[:], in_=bdf[:]), reads=["bdf"], writes=["bdv"])
                k.op("pool", lambda e: e.memset(zero[:], 0.0), writes=["zero"])
                k.op("pool", lambda e: e.memset(epsc[:], EPS), writes=["epsc"])
                for h in range(4):
                    k.op("pool", lambda e, h=h: e.memset(Cf[h][:], 0.0), writes=[f"Cf{h}_{i}" for i in range(4)])
                    k.op("pool", lambda e, h=h: e.memset(Cb[h][:], 0.0), writes=[f"Cb{h}_{i}" for i in range(4)])
                k.op("pool", lambda e: e.memset(nf[:], 0.0), writes=[f"nf{h}" for h in range(4)])
                k.op("pool", lambda e: e.memset(nbf[:], 0.0), writes=[f"nbf{h}" for h in range(4)])
                KS = float(512 ** -0.5)

                def fm_tile(dd, c):
                    return dd[:, c * 128:(c + 1) * 128].rearrange("(fc p) t -> p fc t", p=128)

                def loads(c):
                    b = c % 2
                    k.dma("sp", qT[b][:], fm_tile(qmT_d, c), writes=[f"qT{b}"])
                    k.dma("sp", ksT[b][:], fm_tile(ksT_d, c), writes=[f"ksT{b}"])
                    k.dma("sp", xcb[b][:], fm_tile(xcbT_d, c), writes=[f"xcb{b}"])
                    k.dma("sp", xmb[b][:], fm_tile(xmbT_d, c), writes=[f"xmb{b}"])
                    k.dma("sp", om[b][:], om_d[c * 128:(c + 1) * 128, :], writes=[f"om{b}"])
                    k.dma("sp", zmT[b][:], fm_tile(zmT_d, c), writes=[f"zmT{b}"])
                    k.dma("sp", sxT[b][:], fm_tile(sxT_d, c), writes=[f"sxT{b}"])

                loads(0)
                NCH = int(os.environ.get('MC_NCH', '32'))
                units = [(c, h) for c in range(NCH) for h in range(4)]

                def stageA1(u):
                    c, h = units[u]
                    b = c % 2
                    ub = u % 3
                    if h == 0:
                        k.op("dve", lambda e: e.tensor_copy(out=ucol[b][:, 0:4], in_=G[:, c, 0:4]),
                             reads=["G"], writes=[f"ucol{b}"])
                        k.op("dve", lambda e: e.tensor_copy(out=ucol[b][:, 4:8], in_=G[:, c, 12:16]),
                             reads=["G"], writes=[f"ucol{b}"])
                    fcs = [4 * h + i for i in range(4)]
                    for i, fc in enumerate(fcs):
                        k.op("pe", lambda e, i=i, fc=fc: e.matmul(
                            pk[:, i * 128:(i + 1) * 128], lhsT=xcb[b][:, fc, :], rhs=bdk[:, fc, :],
                            start=True, stop=True), reads=[f"xcb{b}", "bdk"], writes=["pk"])
                    for i, fc in enumerate(fcs):
                        k.op("pe", lambda e, i=i, fc=fc: e.matmul(
                            pv[:, i * 128:(i + 1) * 128], lhsT=xmb[b][:, fc, :], rhs=bdv[:, fc, :],
                            start=True, stop=True), reads=[f"xmb{b}", "bdv"], writes=["pv"])
                    pS = pm[:, ub * 128:(ub + 1) * 128]
                    for i, fc in enumerate(fcs):
                        k.op("pe", lambda e, i=i, fc=fc: e.matmul(
                            pS, lhsT=ksT[b][:, fc, :], rhs=qT[b][:, fc, :], start=(i == 0), stop=(i == 3)),
                            reads=[f"ksT{b}", f"qT{b}"], writes=[f"pS{ub}", "pmB"])
                    k.op("act", lambda e: e.activation(out=ktm[ub][:], in_=pk[:], func=AF.Identity, scale=KS),
                         reads=["pk"], writes=[f"ktm{ub}"])
                    k.op("act", lambda e: e.activation(
                        out=v1[ub][:], in_=pv[:], func=AF.Identity, scale=G[:, c, h:h + 1]),
                        reads=["pv", "G"], writes=[f"v1{ub}"])
                    k.op("act", lambda e: e.activation(
                        out=v2[ub][:], in_=pv[:], func=AF.Identity, scale=G[:, c, 12 + h:13 + h]),
                        reads=["pv", "G"], writes=[f"v2{ub}"])
                    k.op("dve", lambda e: e.tensor_tensor(out=Pm[ub][:], in0=pS, in1=tri[:], op=ALU.mult),
                         reads=[f"pS{ub}", "tri", "pmB"], writes=[f"Pm{ub}"])

                def stageA2(u):
                    c, h = units[u]
                    b = c % 2
                    ub = u % 3
                    nb_ = u % 2
                    fcs = [4 * h + i for i in range(4)]
                    for i, fc in enumerate(fcs):
                        k.op("pe", lambda e, i=i, fc=fc: e.matmul(
                            pN[nb_][:], lhsT=qT[b][:, fc, :], rhs=Cb[h][:, i, :], start=(i == 0), stop=False),
                            reads=[f"qT{b}", f"Cb{h}_{i}"], writes=[f"pN{nb_}"])
                    k.op("pe", lambda e: e.matmul(pN[nb_][:], lhsT=Pm[ub][:], rhs=v1[ub][:], start=False, stop=True),
                         reads=[f"Pm{ub}", f"v1{ub}"], writes=[f"pN{nb_}"])
                    pd = pm[:, 400 + h:401 + h]
                    for i, fc in enumerate(fcs):
                        k.op("pe", lambda e, i=i, fc=fc: e.matmul(
                            pd, lhsT=qT[b][:, fc, :], rhs=nbf[:, h, i:i + 1], start=(i == 0), stop=False),
                            reads=[f"qT{b}", f"nbf{h}"], writes=[f"pd{h}", "pmB"])
                    k.op("pe", lambda e: e.matmul(
                        pd, lhsT=Pm[ub][:], rhs=ucol[b][:, h:h + 1], start=False, stop=True),
                        reads=[f"Pm{ub}", f"ucol{b}"], writes=[f"pd{h}", "pmB"])
                    dec = Gl[:, c, 4 + h:5 + h]
                    for i in range(4):
                        kvs = (u * 4 + i) % 2
                        k.op("pe", lambda e, i=i, kvs=kvs: e.matmul(
                            pkv[kvs][:], lhsT=ktm[ub][:, i * 128:(i + 1) * 128], rhs=v2[ub][:], start=True, stop=True),
                            reads=[f"ktm{ub}", f"v2{ub}"], writes=[f"pkv{kvs}"])
                        pkn = pm[:, 416 + 4 * h + i:417 + 4 * h + i]
                        k.op("pe", lambda e, i=i, pkn=pkn: e.matmul(
                            pkn, lhsT=ktm[ub][:, i * 128:(i + 1) * 128], rhs=ucol[b][:, 4 + h:5 + h],
                            start=True, stop=True),
                            reads=[f"ktm{ub}", f"ucol{b}"], writes=[f"pkn{h}", "pmB"])
                        k.op("dve", lambda e, i=i, kvs=kvs: e.scalar_tensor_tensor(
                            out=Cf[h][:, i, :], in0=Cf[h][:, i, :], scalar=dec, in1=pkv[kvs][:],
                            op0=ALU.mult, op1=ALU.add),
                            reads=[f"Cf{h}_{i}", "Gl", f"pkv{kvs}"], writes=[f"Cf{h}_{i}"])
                        k.op("act", lambda e, i=i: e.copy(out=Cb[h][:, i, :], in_=Cf[h][:, i, :]),
                             reads=[f"Cf{h}_{i}"], writes=[f"Cb{h}_{i}"])
                    k.op("dve", lambda e: e.scalar_tensor_tensor(
                        out=nf[:, h, :], in0=nf[:, h, :], scalar=dec, in1=pm[:, 416 + 4 * h:420 + 4 * h],
                        op0=ALU.mult, op1=ALU.add),
                        reads=[f"nf{h}", "Gl", f"pkn{h}", "pmB"], writes=[f"nf{h}"])
                    k.op("dve", lambda e: e.tensor_copy(out=nbf[:, h, :], in_=nf[:, h, :]),
                         reads=[f"nf{h}"], writes=[f"nbf{h}"])

                def stageB(u):
                    c, h = units[u]
                    b = c % 2
                    ub = u % 2
                    u3 = u % 3
                    s_ = sm[u3]
                    smn = f"sm{u3}"
                    pd = pm[:, 400 + h:401 + h]
                    wq = G[:, c, 4 + h:5 + h]
                    ebq = G[:, c, 8 + h:9 + h]
                    k.op("dve", lambda e: e.tensor_scalar(
                        out=s_[:, 14:15], in0=pd, scalar1=wq, scalar2=None, op0=ALU.mult),
                        reads=[f"pd{h}", "G", "pmB"], writes=[smn])
                    k.op("dve", lambda e: e.scalar_tensor_tensor(
                        out=s_[:, 0:1], in0=s_[:, 14:15], scalar=-1.0, in1=s_[:, 14:15],
                        op0=ALU.mult, op1=ALU.max),
                        reads=[smn], writes=[smn])
                    k.op("dve", lambda e: e.tensor_tensor(out=s_[:, 1:2], in0=s_[:, 0:1], in1=ebq, op=ALU.max),
                         reads=[smn, "G"], writes=[smn])
                    k.op("dve", lambda e: e.reciprocal(out=s_[:, 2:3], in_=s_[:, 1:2]), reads=[smn], writes=[smn])
                    k.op("dve", lambda e: e.tensor_tensor(out=s_[:, 3:4], in0=s_[:, 2:3], in1=wq, op=ALU.mult),
                         reads=[smn, "G"], writes=[smn])
                    k.op("dve", lambda e: e.scalar_tensor_tensor(
                        out=hs[u3][:], in0=pN[ub][:], scalar=s_[:, 3:4], in1=om[b][:, h * 512:(h + 1) * 512],
                        op0=ALU.mult, op1=ALU.mult),
                        reads=[f"pN{ub}", smn, f"om{b}"], writes=[f"hs{u3}"])
                    k.op("dve", lambda e: e.bn_stats(out=s_[:, 4:10], in_=hs[u3][:]), reads=[f"hs{u3}"], writes=[smn])
                    k.op("dve", lambda e: e.bn_aggr(out=s_[:, 10:12], in_=s_[:, 4:10]), reads=[smn], writes=[smn])
                    k.op("act", lambda e: e.activation(out=s_[:, 12:13], in_=s_[:, 11:12], func=AF.Sqrt,
                                                       bias=epsc[:, 0:1]),
                         reads=[smn, "epsc"], writes=[smn])
                    k.op("dve", lambda e: e.reciprocal(out=s_[:, 13:14], in_=s_[:, 12:13]), reads=[smn], writes=[smn])
                    k.op("dve", lambda e: e.scalar_tensor_tensor(
                        out=s_[:, 15:16], in0=s_[:, 10:11], scalar=-1.0, in1=s_[:, 13:14],
                        op0=ALU.mult, op1=ALU.mult),
                        reads=[smn], writes=[smn])
                    k.op("act", lambda e: e.activation(
                        out=hn[u3][:], in_=hs[u3][:], func=AF.Identity, scale=s_[:, 13:14], bias=s_[:, 15:16]),
                        reads=[f"hs{u3}", smn], writes=[f"hn{u3}"])
                    k.op("dve", lambda e: e.tensor_tensor(
                        out=hg[u3][:], in0=hn[u3][:], in1=hng[:, h * 512:(h + 1) * 512], op=ALU.mult),
                        reads=[f"hn{u3}", "hng"], writes=[f"hg{u3}"])

                def stageC(u):
                    c, h = units[u]
                    b = c % 2
                    u3 = u % 3
                    for i in range(4):
                        k.op("pe", lambda e, i=i: e.transpose(
                            out=pT[:, i, :], in_=hg[u3][:, i * 128:(i + 1) * 128], identity=idf[:]),
                            reads=[f"hg{u3}", "idf"], writes=["pT"])
                    k.op("dve", lambda e: e.tensor_tensor(
                        out=tt[u3][:], in0=pT[:], in1=sxT[b][:, 4 * h:4 * h + 4, :], op=ALU.add),
                        reads=["pT", f"sxT{b}"], writes=[f"tt{u3}"])
                    k.op("dve", lambda e: e.tensor_tensor(
                        out=hzs[b][:, 4 * h:4 * h + 4, :], in0=tt[u3][:], in1=zmT[b][:, 4 * h:4 * h + 4, :],
                        op=ALU.mult),
                        reads=[f"tt{u3}", f"zmT{b}"], writes=[f"hzs{b}_{h}"])
                    if h == 3:
                        k.dma("sp", fm_tile(hzT_d, c), hzs[b][:], reads=[f"hzs{b}_{hh}" for hh in range(4)])

                NU = len(units)
                for idx in range(NU + 3):
                    lists = []
                    if idx < NU:
                        lists.append(k.collect(stageA1, idx))
                    if 0 <= idx - 1 < NU:
                        lists.append(k.collect(stageA2, idx - 1))
                    if 0 <= idx - 2 < NU:
                        lists.append(k.collect(stageB, idx - 2))
                    if 0 <= idx - 3 < NU:
                        lists.append(k.collect(stageC, idx - 3))
                    if os.environ.get("MC_INTERLEAVE", "1") == "1":
                        k.emit_interleaved(lists)
                    else:
                        for l in lists:
                            for it in l:
                                k.op(*it)
                    if idx % 4 == 2:
                        cn = idx // 4 + 1
                        if cn < 32:
                            loads(cn)
                k.barrier()
                k.flush()
        if "p3" in phases:
            with ExitStack() as ph:
                def sb(name, shape, dt):
                    return ph.enter_context(nc.sbuf_tensor("sb6_" + name, list(shape), dt))

                def ps(name, shape, dt=F32):
                    return ph.enter_context(nc.psum_tensor("ps6_" + name, list(shape), dt))

                wpa = sb("wpa", [128, 8, 1024], BF16)
                wpb = sb("wpb", [128, 16, 1024], BF16)
                wout = sb("wout", [128, 8, 1024], BF16)
                gout = sb("gout", [128, 1024], F32)
                ident = sb("ident", [128, 128], BF16)
                identf = sb("identf", [128, 128], F32)
                epsc = sb("epsc", [128, 1], F32)
                hzT = [sb(f"hzT{i}", [128, 16, 512], BF16) for i in range(2)]
                gr = [sb(f"gr{i}", [128, 512], F32) for i in range(4)]
                at = [sb(f"at{i}", [128, 16, 65], F32) for i in range(6)]
                za = [sb(f"za{i}", [128, 1024], F32) for i in range(2)]
                xt = [sb(f"xt{i}", [128, 1024], F32) for i in range(2)]
                rden = [sb(f"rden{i}", [128, 16], F32) for i in range(2)]
                yazb = [sb(f"yazb{i}", [128, 1024], BF16) for i in range(2)]
                yazT = [sb(f"yazT{i}", [128, 8, 512], BF16) for i in range(2)]
                m1 = [sb(f"m1{i}", [128, 512], F32) for i in range(2)]
                m2 = [sb(f"m2{i}", [128, 512], F32) for i in range(2)]
                mT = [sb(f"mT{i}", [128, 8, 512], BF16) for i in range(2)]
                hh = [sb(f"hh{i}", [128, 1024], F32) for i in range(2)]
                sq = sb("sq", [128, 1024], F32)
                st = sb("st", [128, 64], F32)
                psT = ps("T", [128, 1024], BF16)
                pya = [ps(f"ya{i}", [128, 512]) for i in range(2)]
                pym = [ps(f"ym{i}", [128, 512]) for i in range(2)]
                po = [ps(f"o{i}", [128, 512]) for i in range(2)]

                k.dma("pool", wpa[:], wpa_d.rearrange("(kc p) c -> p kc c", p=128), writes=["wpa"])
                k.dma("pool", wpb[:], wpb_d.rearrange("(kc p) c -> p kc c", p=128), writes=["wpb"])
                k.dma("pool", wout[:], wout_d.rearrange("(kc p) c -> p kc c", p=128), writes=["wout"])
                k.dma("sp", gout[:], gout_d, writes=["gout"])
                k.dma("sp", identf[:], ident_d, writes=["identf"])
                k.op("dve", lambda e: e.tensor_copy(out=ident[:], in_=identf[:]), reads=["identf"], writes=["ident"])
                k.op("pool", lambda e: e.memset(epsc[:], EPS), writes=["epsc"])

                def fm_grp(dd, g):
                    return dd[:, g * 512:(g + 1) * 512].rearrange("(fc p) t -> p fc t", p=128)

                state = {"gri": 0, "tix": 0, "t1": 0}

                def s1_tile(g, tt4):
                    t = g * 4 + tt4
                    b = t % 2
                    yz = yazT[g % 2]
                    yzn = f"yazT{g % 2}"
                    rows = slice(t * 128, (t + 1) * 128)
                    ats = [at[3 * b + p_] for p_ in range(3)]
                    atn = [f"at{3 * b + p_}" for p_ in range(3)]
                    for p_ in range(3):
                        k.dma("sp", ats[p_][:], att_d[p_][rows, :, :], writes=[atn[p_]])
                    k.dma("sp", za[b][:], za_d[rows, :], writes=[f"za{b}"])
                    k.op("dve", lambda e: e.tensor_tensor(out=ats[0][:], in0=ats[0][:], in1=ats[1][:], op=ALU.add),
                         reads=[atn[0], atn[1]], writes=[atn[0]])
                    k.op("dve", lambda e: e.tensor_tensor(out=ats[0][:], in0=ats[0][:], in1=ats[2][:], op=ALU.add),
                         reads=[atn[0], atn[2]], writes=[atn[0]])
                    k.op("dve", lambda e: e.reciprocal(out=rden[b][:], in_=ats[0][:, :, 64]),
                         reads=[atn[0]], writes=[f"rden{b}"])
                    for h in range(16):
                        k.op("dve", lambda e, h=h: e.scalar_tensor_tensor(
                            out=yazb[b][:, h * 64:(h + 1) * 64], in0=ats[0][:, h, 0:64], scalar=rden[b][:, h:h + 1],
                            in1=za[b][:, h * 64:(h + 1) * 64], op0=ALU.mult, op1=ALU.mult),
                            reads=[atn[0], f"rden{b}", f"za{b}"], writes=[f"yazb{b}_{h}"])
                    for c in range(8):
                        k.op("pe", lambda e, c=c: e.transpose(
                            out=psT[:, c * 128:(c + 1) * 128], in_=yazb[b][:, c * 128:(c + 1) * 128],
                            identity=ident[:]),
                            reads=[f"yazb{b}_{2 * c}", f"yazb{b}_{2 * c + 1}", "ident"], writes=["psT"])
                    k.op("act", lambda e: e.copy(
                        out=yz[:, :, tt4 * 128:(tt4 + 1) * 128],
                        in_=psT[:].rearrange("p (c t) -> p c t", c=8)),
                        reads=["psT"], writes=[yzn])

                def s2_chunk(g, j):
                    hb = g % 2
                    pb = j % 2
                    yz = yazT[g % 2]
                    yzn = f"yazT{g % 2}"
                    gri = state["gri"]
                    ga = gr[gri % 4]
                    gan = f"gr{gri % 4}"
                    gm = gr[(gri + 1) % 4]
                    gmn = f"gr{(gri + 1) % 4}"
                    state["gri"] = gri + 2
                    k.dma("sp", ga[:], gT_d[j * 128:(j + 1) * 128, g * 512:(g + 1) * 512], writes=[gan])
                    k.dma("sp", gm[:], gT_d[1024 + j * 128:1024 + (j + 1) * 128, g * 512:(g + 1) * 512],
                          writes=[gmn])
                    for kc in range(8):
                        k.op("pe", lambda e, kc=kc: e.matmul(
                            pya[pb][:], lhsT=wpa[:, kc, j * 128:(j + 1) * 128], rhs=yz[:, kc, :],
                            start=(kc == 0), stop=(kc == 7)),
                            reads=["wpa", yzn], writes=[f"pya{pb}"])
                    for kc in range(16):
                        k.op("pe", lambda e, kc=kc: e.matmul(
                            pym[pb][:], lhsT=wpb[:, kc, j * 128:(j + 1) * 128], rhs=hzT[hb][:, kc, :],
                            start=(kc == 0), stop=(kc == 15)),
                            reads=["wpb", f"hzT{hb}"], writes=[f"pym{pb}"])
                    k.op("dve", lambda e: e.tensor_tensor(out=m1[pb][:], in0=pya[pb][:], in1=ga[:], op=ALU.mult),
                         reads=[f"pya{pb}", gan], writes=[f"m1{pb}"])
                    k.op("dve", lambda e: e.tensor_tensor(out=m2[pb][:], in0=pym[pb][:], in1=gm[:], op=ALU.mult),
                         reads=[f"pym{pb}", gmn], writes=[f"m2{pb}"])
                    k.op("dve", lambda e: e.tensor_tensor(out=mT[g % 2][:, j, :], in0=m1[pb][:], in1=m2[pb][:], op=ALU.add),
                         reads=[f"m1{pb}", f"m2{pb}"], writes=[f"mT{g % 2}_{j}"])

                def s3_tile(g, tt4):
                    t = g * 4 + tt4
                    b = t % 2
                    rows = slice(t * 128, (t + 1) * 128)
                    mT_all = [f"mT{g % 2}_{j}" for j in range(8)]
                    k.dma("sp", xt[b][:], x_d[rows, :], writes=[f"xt{b}"])
                    for half in range(2):
                        for kc in range(8):
                            k.op("pe", lambda e, half=half, kc=kc: e.matmul(
                                po[half][:], lhsT=mT[g % 2][:, kc, tt4 * 128:(tt4 + 1) * 128],
                                rhs=wout[:, kc, half * 512:(half + 1) * 512], start=(kc == 0), stop=(kc == 7)),
                                reads=["wout"] + mT_all, writes=[f"po{half}"])
                        k.op("dve", lambda e, half=half: e.tensor_tensor(
                            out=hh[b][:, half * 512:(half + 1) * 512], in0=po[half][:],
                            in1=xt[b][:, half * 512:(half + 1) * 512], op=ALU.add),
                            reads=[f"po{half}", f"xt{b}"], writes=[f"hh{b}"])
                    c_ = state["tix"] % 16
                    state["tix"] += 1
                    k.op("act", lambda e: e.activation(
                        out=sq[:], in_=hh[b][:], func=AF.Square, accum_out=st[:, c_:c_ + 1]),
                        reads=[f"hh{b}"], writes=["sq", f"st{c_}"])
                    k.op("act", lambda e: e.activation(
                        out=st[:, 16 + c_:17 + c_], in_=st[:, c_:c_ + 1], func=AF.Sqrt, scale=1.0 / D,
                        bias=epsc[:, 0:1]),
                        reads=[f"st{c_}", "epsc"], writes=[f"st{16 + c_}"])
                    k.op("dve", lambda e: e.reciprocal(out=st[:, 32 + c_:33 + c_], in_=st[:, 16 + c_:17 + c_]),
                         reads=[f"st{16 + c_}"], writes=[f"st{32 + c_}"])
                    k.op("dve", lambda e: e.scalar_tensor_tensor(
                        out=hh[b][:], in0=hh[b][:], scalar=st[:, 32 + c_:33 + c_], in1=gout[:],
                        op0=ALU.mult, op1=ALU.mult),
                        reads=[f"hh{b}", f"st{32 + c_}", "gout"], writes=[f"hh{b}"])
                    k.dma("sp", out_d[rows, :], hh[b][:], reads=[f"hh{b}"])

                k.dma("sp", hzT[0][:], fm_grp(hzT_d, 0), writes=["hzT0"])
                for tt4 in range(4):
                    s1_tile(0, tt4)
                for g in range(9):
                    hb = g % 2
                    if g + 1 < 8:
                        k.dma("sp", hzT[1 - hb][:], fm_grp(hzT_d, g + 1), writes=[f"hzT{1 - hb}"])
                    for jj in range(4):
                        lists = []
                        if g < 8:
                            lists.append(k.collect(s2_chunk, g, 2 * jj))
                            lists.append(k.collect(s2_chunk, g, 2 * jj + 1))
                        if g + 1 < 8:
                            lists.append(k.collect(s1_tile, g + 1, jj))
                        if g >= 1:
                            lists.append(k.collect(s3_tile, g - 1, jj))
                        k.emit_interleaved(lists)
                k.barrier()
                k.flush()
        k.barrier()
        k.op("sp", lambda e: e.sem_inc(k.sem["sp"], 1))
        k.prog["sp"][-1]["selfinc"] = True
        k.flush()
        if os.environ.get("SIM", "0") == "1":
            print("simulate ok:", k.simulate())
    return nc


def t5_bucket_np(dist):
    f = np.float32
    dist = dist.astype(np.int32)
    large = 16 + (np.log(np.maximum(dist, 16).astype(f) / f(16)) / f(math.log(2048 / 16)) * f(16)).astype(np.int32)
    return np.where(dist < 16, dist, np.minimum(large, 31))


_ATT_CACHE = {}


def attn_tables(rel_bias):
    key = id(rel_bias)
    if key in _ATT_CACHE:
        return _ATT_CACHE[key]
    rel_bias = np.asarray(rel_bias, dtype=np.float32)
    kk = np.arange(128)[:, None, None]
    cc = np.arange(2)[None, :, None]
    qq = np.arange(128)[None, None, :]
    delta = qq + 128 - (kk + 128 * cc)
    mask = ((delta >= 0) & (delta <= 128)).astype(np.float32)
    bt = np.zeros((3, 128, 16, 2, 128), np.float32)
    for p, d in enumerate((1, 4, 16)):
        bucket = t5_bucket_np(np.clip(delta, 0, 128) * d)
        g = rel_bias[bucket]
        bt[p] = np.transpose(g, (0, 3, 1, 2))
    _ATT_CACHE[key] = (np.ascontiguousarray(bt), np.ascontiguousarray(mask))
    return _ATT_CACHE[key]


def block_diag_layout(w):
    w = np.asarray(w, dtype=np.float32).reshape(16, 32, 4, 4)
    out = np.zeros((16, 32, 4, 32, 4), np.float32)
    for gi in range(32):
        out[:, gi, :, gi, :] = w[:, gi]
    out = out.reshape(16, 128, 128).transpose(1, 0, 2)
    return np.ascontiguousarray(out)


def host_inputs(inputs, b):
    f = np.float32
    d = {}
    d["x"] = np.ascontiguousarray(inputs["x"][b], dtype=f)
    d["w_in"] = np.ascontiguousarray(inputs["w_in"][0], dtype=f)
    d["gin_bc"] = np.ascontiguousarray(np.broadcast_to(inputs["norm_in_g"][0][None, :], (128, D)), dtype=f)
    bt, mt = attn_tables(inputs["rel_bias"])
    d["bias_tab"] = bt
    d["mask_tab"] = mt
    d["cw"] = np.ascontiguousarray(inputs["conv_w"][0].reshape(4, 16, 128).transpose(2, 1, 0), dtype=f)
    d["cb"] = np.ascontiguousarray(inputs["conv_b"][0].reshape(16, 128).T, dtype=f)
    d["bif"] = np.ascontiguousarray(inputs["b_if"][0].reshape(8, 1), dtype=f)
    d["wif"] = np.ascontiguousarray(inputs["w_if"][0].reshape(48, 128, 8).transpose(1, 0, 2), dtype=f)
    for nm, key in (("bdq", "wq_m"), ("bdk", "wk_m"), ("bdv", "wv_m")):
        d[nm] = block_diag_layout(inputs[key][0])
    d["w_pa"] = np.ascontiguousarray(inputs["w_pa"][0], dtype=f)
    d["w_pb"] = np.ascontiguousarray(inputs["w_pb"][0], dtype=f)
    d["w_out"] = np.ascontiguousarray(inputs["w_out"][0], dtype=f)
    d["gout_bc"] = np.ascontiguousarray(np.broadcast_to(inputs["norm_out_g"][None, :], (128, D)), dtype=f)
    d["skip_fm"] = np.ascontiguousarray(inputs["skip_m"][0].reshape(16, 128).T, dtype=f)
    d["hng_bc"] = np.ascontiguousarray(np.broadcast_to(inputs["head_norm_g"][0][None, :], (128, 2048)), dtype=f)
    d["tri"] = np.ascontiguousarray(np.triu(np.ones((128, 128), f)))
    sel = np.zeros((128, 128), f)
    sel[127, :] = 1.0
    d["sel127"] = sel
    d["ident"] = np.eye(128, dtype=f)
    d["gate_b_fm"] = np.ascontiguousarray(inputs["gate_b"][0].reshape(16, 128).T, dtype=f)
    return d


def kernel(**inputs):
    inputs = {k_: np.asarray(v) for k_, v in inputs.items()}
    nc = build()
    in_maps = [host_inputs(inputs, b) for b in range(NCORES)]
    res = run_bass_kernel_spmd(nc, in_maps, core_ids=list(range(NCORES)))
    out = np.stack([np.asarray(r["out"]) for r in res.results], axis=0)
    return out.astype(np.float32)
```
